# Optimizing a Trainium2 kernel written in Bass

```python
import jax, jax.numpy as jnp
from jax import lax
import numpy as np

D_MODEL = 1024
BATCH = 2
SEQ = 8192
DEPTH = 1
DEC_BATCH = 1
DEC_SEQ = 16384
PAST_LEN = 128

GRID_W = 64
NA_HEADS = 16
NA_HEAD_DIM = 64
NA_WIDTH = NA_HEADS * NA_HEAD_DIM
NA_KH = 8
NA_KW = 16
RET_HEADS = 4
RET_KEY_DIM = D_MODEL // RET_HEADS
RET_VAL_DIM = 2 * RET_KEY_DIM
RET_QK_WIDTH = RET_HEADS * RET_KEY_DIM
RET_V_WIDTH = RET_HEADS * RET_VAL_DIM
RET_CHUNK = 128
ROPE_BASE = 10000.0
FFN_HIDDEN = ((8 * D_MODEL + 3 * 256 - 1) // (3 * 256)) * 256
IN_WIDTH = 3 * NA_WIDTH + 2 * RET_QK_WIDTH + 2 * RET_V_WIDTH + 2 * D_MODEL
N_MOD = 6
EPS = 1e-6

kernel_name = "hybrid_natten_retention_encoder"


def rmsnorm(x, g):
    xf = x.astype(jnp.float32)
    y = xf * lax.rsqrt(jnp.mean(xf * xf, axis=-1, keepdims=True) + EPS) * g.astype(jnp.float32)
    return y.astype(x.dtype)


def neighbourhood_attention(q, k, v, rpb):
    b, n, h, dh = q.shape
    rows = n // GRID_W
    kh = min(NA_KH, rows)
    kw = NA_KW
    q = q.reshape(b, rows, GRID_W, h, dh) * (dh ** -0.5)
    k = k.reshape(b, rows, GRID_W, h, dh)
    v = v.reshape(b, rows, GRID_W, h, dh)
    cols = np.arange(GRID_W)
    col_start = np.clip(cols - kw // 2, 0, GRID_W - kw)
    col_idx = col_start[:, None] + np.arange(kw)[None, :]
    col_rel = col_idx - cols[:, None] + (NA_KW - 1)
    col_bias = rpb[:, :, col_rel].astype(jnp.float32)

    def row_block(r):
        rs = jnp.clip(r - kh // 2, 0, rows - kh)
        kr = lax.dynamic_slice_in_dim(k, rs, kh, axis=1)
        vr = lax.dynamic_slice_in_dim(v, rs, kh, axis=1)
        kg = kr[:, :, col_idx]
        vg = vr[:, :, col_idx]
        qr = lax.dynamic_index_in_dim(q, r, axis=1, keepdims=False)
        s = jnp.einsum('bchd,bicjhd->bhcij', qr, kg, preferred_element_type=jnp.float32)
        row_rel = rs + jnp.arange(kh) - r + (NA_KH - 1)
        bias = jnp.take(col_bias, row_rel, axis=1)
        s = s + bias.transpose(0, 2, 1, 3)[None]
        p = jax.nn.softmax(s.reshape(b, h, GRID_W, kh * kw), axis=-1).reshape(b, h, GRID_W, kh, kw)
        return jnp.einsum('bhcij,bicjhd->bchd', p.astype(v.dtype), vg)

    out = lax.map(row_block, jnp.arange(rows))
    return out.transpose(1, 0, 2, 3, 4).reshape(b, n, h * dh)


def rotary(x, pos):
    d = x.shape[-1]
    half = d // 2
    inv = 1.0 / (ROPE_BASE ** (jnp.arange(half, dtype=jnp.float32) / half))
    ang = pos[:, None] * inv[None, :]
    cos, sin = jnp.cos(ang), jnp.sin(ang)
    x1, x2 = x[..., :half], x[..., half:]
    return jnp.concatenate([x1 * cos - x2 * sin, x1 * sin + x2 * cos], axis=-1)


def retention_direction(q, k, v, log_gamma, strict):
    b, h, n, dk = q.shape
    dv = v.shape[-1]
    c = RET_CHUNK
    nc = n // c
    pos = jnp.arange(c, dtype=jnp.float32)
    rel = pos[:, None] - pos[None, :]
    mask = rel > 0 if strict else rel >= 0
    lg = log_gamma[:, None, None]
    dmat = jnp.where(mask[None], jnp.exp(jnp.where(mask, rel, 0.0)[None] * lg), 0.0)
    xi = jnp.exp((pos + 1.0)[None, :] * log_gamma[:, None])[None, :, :, None]
    zeta = jnp.exp((c - 1.0 - pos)[None, :] * log_gamma[:, None])[None, :, :, None]
    chunk_decay = jnp.exp(c * log_gamma)[None, :, None, None]

    def to_chunks(t):
        return t.reshape(b, h, nc, c, t.shape[-1]).transpose(2, 0, 1, 3, 4)

    def step(state, xs):
        qc, kc, vc = xs
        scores = jnp.einsum('bhqd,bhkd->bhqk', qc, kc) * dmat[None]
        intra = jnp.einsum('bhqk,bhkv->bhqv', scores, vc)
        cross = jnp.einsum('bhqd,bhdv->bhqv', qc, state) * xi
        new_state = state * chunk_decay + jnp.einsum('bhkd,bhkv->bhdv', kc * zeta, vc)
        return new_state, intra + cross

    state0 = jnp.zeros((b, h, dk, dv), jnp.float32)
    _, out = lax.scan(step, state0, (to_chunks(q), to_chunks(k), to_chunks(v)))
    return out.transpose(1, 2, 0, 3, 4).reshape(b, h, n, dv)


def bidirectional_retention(q, k, v, decay_logit, gn_gain):
    b, n, h, dk = q.shape
    pos = jnp.arange(n, dtype=jnp.float32)
    qf = rotary(q.astype(jnp.float32).transpose(0, 2, 1, 3), pos)
    kf = rotary(k.astype(jnp.float32).transpose(0, 2, 1, 3), pos) * (dk ** -0.5)
    vf = v.astype(jnp.float32).transpose(0, 2, 1, 3)
    lg = jax.nn.log_sigmoid(decay_logit.astype(jnp.float32))
    fwd = retention_direction(qf, kf, vf, lg[0], False)
    bwd = jnp.flip(retention_direction(jnp.flip(qf, 2), jnp.flip(kf, 2), jnp.flip(vf, 2), lg[1], True), 2)
    y = fwd + bwd
    mu = jnp.mean(y, axis=-1, keepdims=True)
    var = jnp.mean(jnp.square(y - mu), axis=-1, keepdims=True)
    y = (y - mu) * lax.rsqrt(var + EPS)
    y = y.transpose(0, 2, 1, 3).reshape(b, n, h * y.shape[-1]) * gn_gain.astype(jnp.float32)
    return y.astype(q.dtype)


def encoder_layer(x, c, w_mod, b_mod, g_pre_mix, w_in, rpb, ret_decay_logit, ret_gn,
                  w_up_na, w_up_ret, w_out, g_post_mix, g_pre_ffn, w_ffn_in, w_ffn_out, g_post_ffn):
    b, n, d = x.shape
    mod = jnp.dot(jax.nn.silu(c), w_mod) + b_mod
    sh1, sc1, gt1, sh2, sc2, gt2 = [m[:, None, :] for m in jnp.split(mod, N_MOD, axis=-1)]

    h = rmsnorm(x, g_pre_mix) * (1.0 + sc1) + sh1
    proj = jnp.dot(h, w_in)
    cuts = list(np.cumsum([NA_WIDTH, NA_WIDTH, NA_WIDTH, RET_QK_WIDTH, RET_QK_WIDTH,
                           RET_V_WIDTH, RET_V_WIDTH, D_MODEL])[:])
    na_q, na_k, na_v, r_q, r_k, r_v, r_g, g_na, g_ret = jnp.split(proj, cuts, axis=-1)
    y_na = neighbourhood_attention(na_q.reshape(b, n, NA_HEADS, NA_HEAD_DIM),
                                   na_k.reshape(b, n, NA_HEADS, NA_HEAD_DIM),
                                   na_v.reshape(b, n, NA_HEADS, NA_HEAD_DIM), rpb)
    y_ret = bidirectional_retention(r_q.reshape(b, n, RET_HEADS, RET_KEY_DIM),
                                    r_k.reshape(b, n, RET_HEADS, RET_KEY_DIM),
                                    r_v.reshape(b, n, RET_HEADS, RET_VAL_DIM),
                                    ret_decay_logit, ret_gn)
    y_ret = jax.nn.silu(r_g) * y_ret
    merged = jax.nn.sigmoid(g_na) * jnp.dot(y_na, w_up_na) + jax.nn.sigmoid(g_ret) * jnp.dot(y_ret, w_up_ret)
    out = jnp.dot(merged, w_out)
    x = x + gt1 * rmsnorm(out, g_post_mix)

    h = rmsnorm(x, g_pre_ffn) * (1.0 + sc2) + sh2
    a, gte = jnp.split(jnp.dot(h, w_ffn_in), 2, axis=-1)
    f = jnp.dot(jax.nn.silu(a) * gte, w_ffn_out)
    x = x + gt2 * rmsnorm(f, g_post_ffn)
    return x


def setup_inputs(seed: int = 0) -> dict:
    key = jax.random.key(seed)
    ks = jax.random.split(key, 24)
    f32 = jnp.float32

    def nrm(k, shape, scale):
        return jax.random.normal(k, shape, f32) * scale

    gammas = 1.0 - 2.0 ** (-5.0 - np.arange(RET_HEADS, dtype=np.float32))
    base_logit = jnp.asarray(np.log(gammas / (1.0 - gammas)), f32)
    decay_logit = base_logit[None, None, :] + nrm(ks[4], (DEPTH, 2, RET_HEADS), 0.1)
    return {
        "x_prompt": nrm(ks[0], (BATCH, SEQ, D_MODEL), 1.0),
        "x_sample": nrm(ks[1], (DEC_BATCH, DEC_SEQ, D_MODEL), 1.0),
        "c_prompt": nrm(ks[2], (BATCH, D_MODEL), 1.0),
        "c_sample": nrm(ks[3], (DEC_BATCH, D_MODEL), 1.0),
        "w_mod": nrm(ks[5], (DEPTH, D_MODEL, N_MOD * D_MODEL), 0.5 * D_MODEL ** -0.5),
        "b_mod": nrm(ks[6], (DEPTH, N_MOD * D_MODEL), 0.02),
        "g_pre_mix": 1.0 + nrm(ks[7], (DEPTH, D_MODEL), 0.05),
        "w_in": nrm(ks[8], (DEPTH, D_MODEL, IN_WIDTH), D_MODEL ** -0.5),
        "rpb": nrm(ks[9], (DEPTH, NA_HEADS, 2 * NA_KH - 1, 2 * NA_KW - 1), 0.5),
        "ret_decay_logit": decay_logit,
        "ret_gn": 1.0 + nrm(ks[10], (DEPTH, RET_V_WIDTH), 0.05),
        "w_up_na": nrm(ks[11], (DEPTH, NA_WIDTH, D_MODEL), NA_WIDTH ** -0.5),
        "w_up_ret": nrm(ks[12], (DEPTH, RET_V_WIDTH, D_MODEL), RET_V_WIDTH ** -0.5),
        "w_out": nrm(ks[13], (DEPTH, D_MODEL, D_MODEL), D_MODEL ** -0.5),
        "g_post_mix": 1.0 + nrm(ks[14], (DEPTH, D_MODEL), 0.05),
        "g_pre_ffn": 1.0 + nrm(ks[15], (DEPTH, D_MODEL), 0.05),
        "w_ffn_in": nrm(ks[16], (DEPTH, D_MODEL, 2 * FFN_HIDDEN), D_MODEL ** -0.5),
        "w_ffn_out": nrm(ks[17], (DEPTH, FFN_HIDDEN, D_MODEL), FFN_HIDDEN ** -0.5),
        "g_post_ffn": 1.0 + nrm(ks[18], (DEPTH, D_MODEL), 0.05),
    }


def reference(x_prompt, x_sample, c_prompt, c_sample, w_mod, b_mod, g_pre_mix, w_in, rpb,
              ret_decay_logit, ret_gn, w_up_na, w_up_ret, w_out, g_post_mix, g_pre_ffn,
              w_ffn_in, w_ffn_out, g_post_ffn):
    y_prompt = x_prompt
    y_sample = x_sample
    for l in range(DEPTH):
        layer_w = (w_mod[l], b_mod[l], g_pre_mix[l], w_in[l], rpb[l], ret_decay_logit[l], ret_gn[l],
                   w_up_na[l], w_up_ret[l], w_out[l], g_post_mix[l], g_pre_ffn[l], w_ffn_in[l],
                   w_ffn_out[l], g_post_ffn[l])
        y_prompt = encoder_layer(y_prompt, c_prompt, *layer_w)
        y_sample = encoder_layer(y_sample, c_sample, *layer_w)
    return (y_prompt, y_sample)
```

```python
from contextlib import ExitStack
import os
import numpy as np
import ml_dtypes
import concourse.bass as bass
import concourse.mybir as mybir
from concourse.bass_utils import run_bass_kernel_spmd

F32 = mybir.dt.float32
BF16 = mybir.dt.bfloat16
AF = mybir.ActivationFunctionType
ALU = mybir.AluOpType

P = 128
D = 1024
TOK = 4096
HALO = 256
EXT = TOK + 2 * HALO
NT = TOK // P
NTE = EXT // P
INW = 11264
FFH = 2816
EPS = 1e-6
NEG = -30000.0
NCORES = 8

NA_CLASS_DELTAS = {0: [-2, -1, 0, 1, 2, 3], 1: [-2, -1, 0, 1, 2], 2: [-2, -1, 0, 1, 2],
                   3: [-2, -1, 0, 1, 2], 4: [-3, -2, -1, 0, 1, 2]}


def na_class(p):
    return {0: 0, 1: 1, NT - 2: 3, NT - 1: 4}.get(p, 2)


class Emit:
    def __init__(self, nc, es):
        self.nc = nc
        self.E = {'pe': nc.tensor, 'act': nc.scalar, 'dve': nc.vector, 'pool': nc.gpsimd, 'sp': nc.sync}
        self.sem = {e: es.enter_context(nc.semaphore("sem_" + e)) for e in self.E}
        self.cnt = {e: 0 for e in self.E}
        self.seen = {e: {} for e in self.E}
        self.ND = 56
        self.dsem = [es.enter_context(nc.semaphore("dsem%d" % i)) for i in range(self.ND)]
        self.dval = [0] * self.ND
        self.dpool = {'sp': list(range(0, 36)), 'pool': list(range(36, 56))}
        self.dnext = {'sp': 0, 'pool': 0}
        self.W = {}
        self.R = {}

    def _semobj(self, key):
        return self.sem[key] if isinstance(key, str) else self.dsem[key]

    def _wait(self, e, deps):
        best = {}
        for (k, v) in deps:
            if k == 'pe' and e == 'pe':
                continue
            if v > best.get(k, 0):
                best[k] = v
        for k, v in best.items():
            if self.seen[e].get(k, 0) < v:
                self.E[e].wait_ge(self._semobj(k), v)
                self.seen[e][k] = v

    def _deps(self, r, w, e=None):
        deps = []
        for k in r:
            if k in self.W:
                deps.append(self.W[k])
        for k in w:
            if k in self.W and self.W[k][0] != e:
                deps.append(self.W[k])
            deps += [d for d in self.R.get(k, []) if d[0] != e]
        return deps

    def _record(self, me, r, w):
        for k in r:
            self.R.setdefault(k, []).append(me)
        for k in w:
            self.W[k] = me
            self.R[k] = []

    def grp(self, e, fns, r=(), w=()):
        self._wait(e, self._deps(r, w, e if e in ('act', 'dve', 'pool', 'pe') else None))
        eng = self.E[e]
        ins = None
        for fn in fns:
            ins = fn(eng)
        self.cnt[e] += 1
        ins.then_inc(self.sem[e], 1)
        self._record((e, self.cnt[e]), r, w)

    def op(self, e, fn, r=(), w=()):
        self.grp(e, [fn], r, w)

    def dma(self, q, out, in_, r=(), w=(), **kw):
        self._wait(q, self._deps(r, w))
        pool_ = self.dpool[q]
        i = pool_[self.dnext[q]]
        self.dnext[q] = (self.dnext[q] + 1) % len(pool_)
        self.dval[i] += 16
        self.E[q].dma_start(out=out, in_=in_, **kw).then_inc(self.dsem[i], 16)
        self._record((i, self.dval[i]), r, w)

    def barrier(self):
        deps = [(e, self.cnt[e]) for e in self.E if self.cnt[e] > 0]
        deps += [(i, self.dval[i]) for i in range(self.ND) if self.dval[i] > 0]
        for e in self.E:
            self._wait_all(e, deps)
        self.W = {}
        self.R = {}

    def _wait_all(self, e, deps):
        for (k, v) in deps:
            if k == e:
                continue
            if self.seen[e].get(k, 0) < v:
                self.E[e].wait_ge(self._semobj(k), v)
                self.seen[e][k] = v


def build(stop=None, mode='main'):
    nc = bass.Bass('TRN2', target_bir_lowering=False)
    SK = dict(kind="ExternalOutput") if stop else {}

    def din(name, shape, dt=F32):
        return nc.dram_tensor(name, list(shape), dt, kind="ExternalInput").ap()

    x_ext = din("x_ext", [EXT, D])
    cT = din("cT", [P, 8])
    w_mod = din("w_mod", [D, 6 * D])
    b_mod = din("b_mod", [1, 6 * D])
    g_rows = din("g_rows", [1, 4 * D])
    w_in = din("w_in", [D, INW])
    comb_in = din("comb_in", [16, P, 5, 6, P])
    dlog = din("dlog", [P, 8])
    gn_rep = din("gn_rep", [P, 2048])
    w_up_na = din("w_up_na", [D, D])
    w_up_ret = din("w_up_ret", [2048, D])
    w_out = din("w_out", [D, D])
    w_ffn_in = din("w_ffn_in", [D, 2 * FFH])
    w_ffn_out = din("w_ffn_out", [FFH, D])
    cos_t = din("cos_t", [P, TOK])
    sin_t = din("sin_t", [P, TOK])
    ident_in = din("ident", [P, P], BF16)
    dconst = din("dconst", [P, 4, P])
    xiexp = din("xiexp", [P, 2, 512])
    zexp = din("zexp", [P, 2])
    wexp = din("wexp", [P, 2, NT])
    ccd = din("ccd", [P, 16])
    ccv = din("ccv", [P, 16])
    y_out = nc.dram_tensor("y_out", [TOK, D], F32, kind="ExternalOutput").ap()
    if mode == 'pre':
        sloc_out = nc.dram_tensor("sloc_out", [4, P, 2048], F32, kind="ExternalOutput").ap()
    if mode == 'main':
        gath_in = din("gath_in", [4, 4 * P, 2048])

    hT_d = nc.dram_tensor("hT_d", [8, P, TOK], BF16, **(SK if stop == "B" else {})).ap()
    ynaT_d = nc.dram_tensor("ynaT_d", [8, P, TOK], BF16, **(SK if stop == "C" else {})).ap()
    yretT_d = nc.dram_tensor("yretT_d", [16, P, TOK], BF16, **(SK if stop == "D" else {})).ap()
    sb_d = nc.dram_tensor("sb_d", [NT, P, 1024], BF16).ap()
    sloc_h = [nc.dram_tensor("sloc_d%d" % h, [P, 2048], F32) for h in range(4)]
    gath_h = [nc.dram_tensor("gath_d%d" % h, [4 * P, 2048], F32) for h in range(4)]
    x1_d = nc.dram_tensor("x1_d", [TOK, D], F32, **(SK if stop == "Ea" else {})).ap()
    h2T_d = nc.dram_tensor("h2T_d", [8, P, TOK], BF16, **(SK if stop == "Ea" else {})).ap()
    gg_d = nc.dram_tensor("gg_d", [2, P, D], F32, **(SK if stop == "A" else {})).ap()

    es = ExitStack()
    with es:
        k = Emit(nc, es)
        cc_sem = es.enter_context(nc.semaphore("cc_sem"))

        def sb(name, shape, dt=F32, stack=es):
            return stack.enter_context(nc.sbuf_tensor(name, list(shape), dt))

        def ps(name, shape, dt=F32, stack=es):
            return stack.enter_context(nc.psum_tensor(name, list(shape), dt))

        ident = sb("ident_sb", [P, P], BF16)
        ones_r = sb("ones_r", [1, P], F32)
        gs1 = sb("gs1", [P, 8]); sh1 = sb("sh1", [P, 8])
        gs2 = sb("gs2", [P, 8]); sh2 = sb("sh2", [P, 8])
        nhalf = sb("nhalf", [P, 1])
        k.dma('sp', ident[:], ident_in, w=['ident'])
        k.op('dve', lambda e: e.memset(ones_r[:], 1.0), w=['ones_r'])
        k.op('dve', lambda e: e.memset(nhalf[:], -0.5), w=['nhalf'])

        def rstd_from_ss(ss, rstd, key_ss, key_rstd, scale):
            k.op('dve', lambda e: e.tensor_scalar(out=rstd, in0=ss, scalar1=scale, scalar2=EPS,
                                                  op0=ALU.mult, op1=ALU.add), r=[key_ss], w=[key_rstd])
            k.op('pool', lambda e: e.tensor_tensor(out=rstd, in0=rstd, in1=nhalf[:], op=ALU.pow),
                 r=[key_rstd, 'nhalf'], w=[key_rstd])

        with ExitStack() as st:
            csb = sb("csb", [P, 8], F32, st)
            sil = sb("sil", [P, 8], F32, st)
            wm = [sb("wm%d" % i, [P, 8, 512], F32, st) for i in range(2)]
            brow = sb("brow", [1, 6 * D], F32, st)
            grow = sb("grow", [1, 4 * D], F32, st)
            mrow = sb("mrow", [1, 6 * D], F32, st)
            r1 = sb("r1", [P, 4 * D], F32, st)
            E0 = sb("E0", [P, P], F32, st)
            e0 = sb("e0c", [P, 1], F32, st)
            k.op('dve', lambda e: e.memset(r1[:], 0.0), w=['r1'])
            k.op('dve', lambda e: e.memset(E0[:], 0.0), w=['E0'])
            k.op('dve', lambda e: e.memset(E0[0:1, :], 1.0), w=['E0'])
            k.op('dve', lambda e: e.memset(e0[:], 0.0), w=['E0'])
            k.op('dve', lambda e: e.memset(e0[0:1, :], 1.0), w=['E0'])
            pm = [ps("pm%d" % i, [1, 512], F32, st) for i in range(2)]
            pT = ps("pT", [P, 32], F32, st)
            pB = [ps("pB%d" % i, [P, 512], F32, st) for i in range(2)]
            gg1 = sb("gg1", [P, D], F32, st); gg2 = sb("gg2", [P, D], F32, st)
            k.dma('sp', csb[:], cT, w=['csb'])
            k.dma('sp', brow[:], b_mod, w=['brow'])
            k.dma('sp', grow[:], g_rows, w=['grow'])
            k.op('act', lambda e: e.activation(out=sil[:], in_=csb[:], func=AF.Silu), r=['csb'], w=['sil'])
            if os.environ.get('STOPA') == '0':
                k.barrier(); return nc
            for g in range(12):
                b = g % 2
                k.dma('sp', wm[b][:], w_mod[:, g * 512:(g + 1) * 512].rearrange("(kc p) c -> p kc c", p=P),
                      w=['wm%d' % b])
                k.grp('pe', [(lambda e, kc=kc: e.matmul(out=pm[b][:], lhsT=sil[:, kc:kc + 1], rhs=wm[b][:, kc, :],
                                                          start=(kc == 0), stop=(kc == 7))) for kc in range(8)],
                      r=['sil', 'wm%d' % b], w=['pm%d' % b])
                k.op('dve', lambda e: e.tensor_tensor(out=mrow[:, g * 512:(g + 1) * 512], in0=pm[b][:],
                                                      in1=brow[:, g * 512:(g + 1) * 512], op=ALU.add),
                     r=['pm%d' % b, 'brow'], w=['mrow'])
            if os.environ.get('STOPA') == '1':
                k.barrier(); return nc
            k.op('dve', lambda e: e.scalar_tensor_tensor(out=r1[0:1, 0:D], in0=mrow[:, D:2 * D], scalar=1.0,
                                                         in1=grow[:, 0:D], op0=ALU.add, op1=ALU.mult),
                 r=['mrow', 'grow'], w=['r1'])
            k.op('dve', lambda e: e.tensor_copy(out=r1[0:1, D:2 * D], in_=mrow[:, 0:D]), r=['mrow'], w=['r1'])
            k.op('dve', lambda e: e.scalar_tensor_tensor(out=r1[0:1, 2 * D:3 * D], in0=mrow[:, 4 * D:5 * D], scalar=1.0,
                                                         in1=grow[:, 2 * D:3 * D], op0=ALU.add, op1=ALU.mult),
                 r=['mrow', 'grow'], w=['r1'])
            k.op('dve', lambda e: e.tensor_copy(out=r1[0:1, 3 * D:4 * D], in_=mrow[:, 3 * D:4 * D]), r=['mrow'], w=['r1'])
            if os.environ.get('STOPA') == '2':
                k.barrier(); return nc
            for j in range(32):
                k.op('pe', lambda e, j=j: e.matmul(out=pT[:, j:j + 1], lhsT=r1[:, j * P:(j + 1) * P],
                                                   rhs=e0[:, 0:1], start=True, stop=True),
                     r=['r1', 'E0'], w=['pT'])
            for i, dst in enumerate([gs1, sh1, gs2, sh2]):
                k.op('dve', lambda e, i=i, dst=dst: e.tensor_copy(out=dst[:], in_=pT[:, i * 8:(i + 1) * 8]),
                     r=['pT'], w=['gsv%d' % i])
            if os.environ.get('STOPA') == '3':
                k.barrier(); return nc
            k.op('dve', lambda e: e.tensor_tensor(out=r1[0:1, 0:D], in0=mrow[:, 2 * D:3 * D], in1=grow[:, D:2 * D],
                                                  op=ALU.mult), r=['mrow', 'grow', 'r1'], w=['r1'])
            k.op('dve', lambda e: e.tensor_tensor(out=r1[0:1, D:2 * D], in0=mrow[:, 5 * D:6 * D], in1=grow[:, 3 * D:4 * D],
                                                  op=ALU.mult), r=['mrow', 'grow'], w=['r1'])
            for i, dst in enumerate([gg1, gg2]):
                for hh in range(2):
                    k.op('pe', lambda e, i=i, hh=hh: e.matmul(out=pB[hh][:], lhsT=E0[:, :],
                                                              rhs=r1[:, i * D + hh * 512:i * D + (hh + 1) * 512],
                                                              start=True, stop=True),
                         r=['r1', 'E0'], w=['pB%d' % hh])
                    k.op('dve', lambda e, dst=dst, hh=hh: e.tensor_copy(out=dst[:, hh * 512:(hh + 1) * 512], in_=pB[hh][:]),
                         r=['pB%d' % hh], w=['gg'])
            k.dma('sp', gg_d[0], gg1[:], r=['gg'])
            k.dma('sp', gg_d[1], gg2[:], r=['gg'])
            if stop:
                dbgA = nc.dram_tensor("dbgA", [P, 32], F32, kind="ExternalOutput").ap()
                for i, src_ in enumerate([gs1, sh1, gs2, sh2]):
                    k.dma('sp', dbgA[:, i * 8:(i + 1) * 8], src_[:], r=['gsv%d' % i])
            k.barrier()
            if stop == 'A':
                return nc

        def norm_to_T(xt, key_x, ssb, rsb, xnb, pst, key_ps, gsv, shv, dst_fn, key_dst, tag, defer=None):
            k.op('act', lambda e: e.activation(out=xnb[:], in_=xt, func=AF.Square, accum_out=ssb[:]),
                 r=[key_x], w=['xn' + tag, 'ss' + tag])
            rstd_from_ss(ssb[:], rsb[:], 'ss' + tag, 'rs' + tag, 1.0 / D)
            k.op('act', lambda e: e.activation(out=xnb[:], in_=xt, func=AF.Copy, scale=rsb[:]),
                 r=[key_x, 'rs' + tag], w=['xn' + tag])
            def part2():
                k.grp('pe', [(lambda e, kc=kc: e.transpose(out=pst[:, kc, :], in_=xnb[:, kc * P:(kc + 1) * P], identity=ident[:]))
                             for kc in range(8)], r=['xn' + tag], w=[key_ps])
                for kc in range(8):
                    k.op('dve', lambda e, kc=kc: e.tensor_scalar(out=dst_fn(kc), in0=pst[:, kc, :], scalar1=gsv[:, kc:kc + 1],
                                                                 scalar2=shv[:, kc:kc + 1], op0=ALU.mult, op1=ALU.add),
                         r=[key_ps], w=[key_dst])
            if defer is None:
                part2()
            else:
                defer.append(part2)

        stB = ExitStack()
        hTo = sb("hTo", [P, 8, TOK], BF16, stB)
        stH = ExitStack()
        hTh = sb("hTh", [P, 8, 512], BF16, stH)

        def mt(t):
            return t - 2 if 2 <= t < 2 + NT else (t + 32 if t < 2 else t)

        def hsrc(t):
            m = mt(t)
            return (hTo, m) if m < NT else (hTh, m - NT)

        with ExitStack() as st:
            xb = [sb("xb%d" % i, [P, D], F32, st) for i in range(3)]
            xn = [sb("xnB%d" % i, [P, D], BF16, st) for i in range(2)]
            ssB = [sb("ssB%d" % i, [P, 1], F32, st) for i in range(2)]
            rsB = [sb("rsB%d" % i, [P, 1], F32, st) for i in range(2)]
            pst = [ps("pstB%d" % i, [P, 8, P], BF16, st) for i in range(2)]
            def b_sq(t):
                b3, b2 = t % 3, t % 2
                k.dma('sp', xb[b3][:], x_ext[t * P:(t + 1) * P, :], w=['xb%d' % b3])
                k.op('act', lambda e: e.activation(out=xsq[:], in_=xb[b3][:], func=AF.Square, accum_out=ssB[b2][:]),
                     r=['xb%d' % b3], w=['xsq', 'ssB%d' % b2])
                rstd_from_ss(ssB[b2][:], rsB[b2][:], 'ssB%d' % b2, 'rsB%d' % b2, 1.0 / D)

            def b_rest(t):
                b3, b2 = t % 3, t % 2
                hb_, hm_ = hsrc(t)
                k.op('act', lambda e: e.activation(out=xn[b2][:], in_=xb[b3][:], func=AF.Copy, scale=rsB[b2][:]),
                     r=['xb%d' % b3, 'rsB%d' % b2], w=['xnB%d' % b2])
                k.grp('pe', [(lambda e, kc=kc: e.transpose(out=pst[b2][:, kc, :], in_=xn[b2][:, kc * P:(kc + 1) * P], identity=ident[:]))
                             for kc in range(8)], r=['xnB%d' % b2], w=['pstB%d' % b2])
                for kc in range(8):
                    k.op('dve', lambda e, kc=kc: e.tensor_scalar(out=hb_[:, kc, hm_ * P:(hm_ + 1) * P], in0=pst[b2][:, kc, :],
                                                                 scalar1=gs1[:, kc:kc + 1], scalar2=sh1[:, kc:kc + 1],
                                                                 op0=ALU.mult, op1=ALU.add),
                         r=['pstB%d' % b2], w=[('hT', t)])
                if 2 <= t < 2 + NT and (t - 2) % 4 == 3:
                    o0 = t - 2 - 3
                    k.dma('sp', hT_d[:, :, o0 * P:(o0 + 4) * P].rearrange("kc p t -> p kc t"),
                          hTo[:, :, o0 * P:(o0 + 4) * P], r=[('hT', tt + 2) for tt in range(o0, o0 + 4)])

            xsq = sb("xsqB", [P, D], BF16, st)
            b_sq(0)
            for t in range(NTE):
                if t + 1 < NTE:
                    b_sq(t + 1)
                b_rest(t)
            k.barrier()
        if stop == 'B':
            stH.close(); stB.close()
            return nc

        with ExitStack() as st:
            wq = [sb("wqC%d" % i, [P, 8, P], BF16, st) for i in range(2)]
            wk = [sb("wkC%d" % i, [P, 8, P], BF16, st) for i in range(2)]
            wv = [sb("wvC%d" % i, [P, 8, P], BF16, st) for i in range(2)]
            comb = [sb("combC%d" % i, [P, 2, 5, 6, P], F32, st) for i in range(2)]
            qT = sb("qTC", [P, TOK], BF16, st)
            kT = sb("kTC", [P, EXT], BF16, st)
            vS = sb("vSC", [P, NTE, 2, 65], BF16, st)
            tS = [sb("tSC%d" % i, [P, 6, P], F32, st) for i in range(2)]
            pT_ = [sb("pTC%d" % i, [P, 6, P], BF16, st) for i in range(2)]
            rinv = [sb("rinvC%d" % i, [P, 2], F32, st) for i in range(2)]
            yt = [sb("ytC%d" % i, [P, P], BF16, st) for i in range(2)]
            yT = [sb("yTC%d" % i, [P, TOK], BF16, st) for i in range(2)]
            pproj = [ps("pprojC%d" % i, [P, 512], F32, st) for i in range(2)]
            psA = [ps("psAC%d" % i, [P, 4, P], F32, st) for i in range(2)]
            psB = [ps("psBC%d" % i, [P, 2, P], F32, st) for i in range(2)]
            po = ps("poC", [P, 2, 2, 65], F32, st)
            pyt = ps("pytC", [P, 2, P], BF16, st)
            k.op('pool', lambda e: e.memset(vS[:, :, :, 64:65], 1.0), w=['vS'])

            def load_w(hp):
                b = hp % 2
                for (dst, off, nm) in ((wq, 0, 'wq'), (wk, D, 'wk'), (wv, 2 * D, 'wv')):
                    k.dma('pool', dst[b][:], w_in[:, off + hp * P: off + (hp + 1) * P].rearrange("(kc p) c -> p kc c", p=P),
                          w=['%s%d' % (nm, b)])
                k.dma('sp', comb[b][:], comb_in[2 * hp:2 * hp + 2].rearrange("h p c b q -> p h c b q"), w=['comb%d' % b])

            if mode != 'pre':
                load_w(0)
            for hp in range(8 if mode != 'pre' else 0):
                b = hp % 2
                if hp + 1 < 8:
                    load_w(hp + 1)
                for tb in range(8):
                    pb = tb % 2
                    k.grp('pe', [(lambda e, kc=kc: e.matmul(out=pproj[pb][:], lhsT=wq[b][:, kc, :],
                                                              rhs=hTo[:, kc, tb * 512:(tb + 1) * 512],
                                                              start=(kc == 0), stop=(kc == 7))) for kc in range(8)],
                          r=['wq%d' % b], w=['pproj%d' % pb])
                    k.op('act', lambda e: e.activation(out=qT[:, tb * 512:(tb + 1) * 512], in_=pproj[pb][:], func=AF.Copy,
                                                       scale=0.125), r=['pproj%d' % pb], w=['qT'])
                for tb in range(9):
                    pb = (tb + 1) % 2
                    src_ = hTo[:, :, tb * 512:(tb + 1) * 512] if tb < 8 else hTh[:, :, :]
                    k.grp('pe', [(lambda e, kc=kc: e.matmul(out=pproj[pb][:], lhsT=wk[b][:, kc, :], rhs=src_[:, kc, :],
                                                              start=(kc == 0), stop=(kc == 7))) for kc in range(8)],
                          r=['wk%d' % b], w=['pproj%d' % pb])
                    k.op('act', lambda e: e.activation(out=kT[:, tb * 512:(tb + 1) * 512], in_=pproj[pb][:], func=AF.Copy),
                         r=['pproj%d' % pb], w=['kT'])
                for t4 in range(9):
                    pb = t4 % 2
                    src_ = hTo[:, :, t4 * 512:(t4 + 1) * 512] if t4 < 8 else hTh[:, :, :]
                    fns = []
                    for tt in range(4):
                        for kc in range(8):
                            fns.append(lambda e, tt=tt, kc=kc: e.matmul(
                                out=pproj[pb][:, tt * P:(tt + 1) * P], lhsT=src_[:, kc, tt * P:(tt + 1) * P], rhs=wv[b][:, kc, :],
                                start=(kc == 0), stop=(kc == 7)))
                    k.grp('pe', fns, r=['wv%d' % b], w=['pproj%d' % pb])
                    k.op('dve', lambda e: e.tensor_copy(
                        out=vS[:, t4 * 4:(t4 + 1) * 4, :, 0:64],
                        in_=pproj[pb][:].rearrange("p (t h d) -> p t h d", t=4, h=2)),
                         r=['pproj%d' % pb], w=['vS'])
                pairs = [(p, h) for p in range(NT) for h in range(2)]

                def emit_scores(kk):
                    p, h = pairs[kk]
                    sbi = kk % 2
                    hb = 64 * h
                    dl = NA_CLASS_DELTAS[na_class(p)]
                    fns = []
                    for bi, dlt in enumerate(dl):
                        kt = mt(p + dlt + 2)
                        dst = psA[sbi][:, bi, :] if bi < 4 else psB[sbi][:, bi - 4, :]
                        fns.append(lambda e, dst=dst, kt=kt: e.matmul(
                            out=dst, lhsT=kT[hb:hb + 64, kt * P:(kt + 1) * P], rhs=qT[hb:hb + 64, p * P:(p + 1) * P],
                            start=True, stop=True))
                    k.grp('pe', fns, r=['qT', 'kT'], w=['psA%d' % sbi, 'psB%d' % sbi])

                pend_t = []
                emit_scores(0)
                for kk, (p, h) in enumerate(pairs):
                    if kk + 1 < len(pairs):
                        emit_scores(kk + 1)
                    cl = na_class(p)
                    dl = NA_CLASS_DELTAS[cl]
                    nb = len(dl)
                    ob = p % 2
                    sbi = kk % 2
                    k.op('dve', lambda e: e.tensor_tensor(out=tS[sbi][:, 0:4, :], in0=psA[sbi][:], in1=comb[b][:, h, cl, 0:4, :],
                                                          op=ALU.add), r=['psA%d' % sbi, 'comb%d' % b], w=['tS%d' % sbi])
                    k.op('dve', lambda e: e.tensor_tensor(out=tS[sbi][:, 4:nb, :], in0=psB[sbi][:, 0:nb - 4, :],
                                                          in1=comb[b][:, h, cl, 4:nb, :], op=ALU.add),
                         r=['psB%d' % sbi, 'comb%d' % b], w=['tS%d' % sbi])
                    k.op('act', lambda e: e.activation(out=pT_[sbi][:, 0:nb, :], in_=tS[sbi][:, 0:nb, :], func=AF.Exp),
                         r=['tS%d' % sbi], w=['pT%d' % sbi])
                    fns = []
                    for bi, dlt in enumerate(dl):
                        kt = mt(p + dlt + 2)
                        fns.append(lambda e, bi=bi, kt=kt: e.matmul(
                            out=po[:, ob, h, :], lhsT=pT_[sbi][:, bi, :], rhs=vS[:, kt, h, :],
                            start=(bi == 0), stop=(bi == nb - 1)))
                    k.grp('pe', fns, r=['pT%d' % sbi, 'vS'], w=['po%d' % ob])
                    if h == 1:
                        k.op('dve', lambda e: e.reciprocal(out=rinv[ob][:], in_=po[:, ob, :, 64]), r=['po%d' % ob], w=['rinv%d' % ob])
                        for h2 in range(2):
                            k.op('dve', lambda e, h2=h2: e.tensor_scalar(out=yt[ob][:, h2 * 64:(h2 + 1) * 64], in0=po[:, ob, h2, 0:64],
                                                                         scalar1=rinv[ob][:, h2:h2 + 1], scalar2=None, op0=ALU.mult),
                                 r=['po%d' % ob, 'rinv%d' % ob], w=['yt%d' % ob])
                        def ytr(ob=ob, p=p):
                            k.op('pe', lambda e: e.transpose(out=pyt[:, ob, :], in_=yt[ob][:], identity=ident[:]),
                                 r=['yt%d' % ob], w=['pyt%d' % ob])
                            k.op('act', lambda e: e.activation(out=yT[b][:, p * P:(p + 1) * P], in_=pyt[:, ob, :], func=AF.Copy),
                                 r=['pyt%d' % ob], w=['yT%d' % b])
                        pend_t.append(ytr)
                    elif pend_t:
                        pend_t.pop(0)()
                while pend_t:
                    pend_t.pop(0)()
                k.dma('sp', ynaT_d[hp], yT[b][:], r=['yT%d' % b])
            k.barrier()
        if stop == 'C':
            stH.close(); stB.close()
            return nc

        stH.close()
        with ExitStack() as st:
            lgt = sb("lgt", [P, 8], F32, st)
            dcs = sb("dcs", [P, 4, P], F32, st)
            xie = sb("xie", [P, 2, 512], F32, st)
            zex = sb("zex", [P, 2], F32, st)
            wex = sb("wex", [P, 2, NT], F32, st)
            ccds = sb("ccds", [P, 16], F32, st)
            ccvs = sb("ccvs", [P, 16], F32, st)
            gnr = sb("gnr", [P, 512], F32, st)
            DT = sb("DT", [P, P], F32, st)
            tmpD = sb("tmpD", [P, P], F32, st)
            xiT = sb("xiT", [P, 2, 512], F32, st)
            zT = sb("zT", [P, 2], F32, st)
            wT = sb("wT", [P, 2, NT], F32, st)
            gC = sb("gC", [P, 2], F32, st)
            wcc = sb("wcc", [P, 16], F32, st)
            mark = sb("mark", [P, 1], F32, st)
            wA = sb("wAD", [P, 8, 512], BF16, st)
            wB = sb("wBD", [P, 8, 256], BF16, st)
            kTD = sb("kTD", [P, 2, TOK], BF16, st)
            vD = sb("vD", [P, NT, 512], BF16, st)
            cs_ = sb("csD", [P, 2, 512], F32, st)
            scr = sb("scrD", [P, 4096], F32, st)
            rt = [scr[:, i * 512:(i + 1) * 512] for i in range(4)]
            sloc = scr[:, 0:2048]
            gbuf = scr[:, 2048:4096]
            ktf = [sb("ktfD%d" % i, [P, 256], BF16, st) for i in range(2)]
            ktb = [sb("ktbD%d" % i, [P, 256], BF16, st) for i in range(2)]
            Sst = sb("Sst", [P, 2, 1024], F32, st)
            Sbf = [sb("SbfD%d" % i, [P, 1024], BF16, st) for i in range(2)]
            Sbb = [sb("SbbD%d" % i, [P, 1024], BF16, st) for i in range(2)]
            qTD = sb("qTD", [P, 3, 2, 512], BF16, st)
            Am = [sb("AmD%d" % i, [P, P], BF16, st) for i in range(2)]
            stt = [sb("sttD%d" % i, [P, 6], F32, st) for i in range(2)]
            mv = [sb("mvD%d" % i, [P, 2], F32, st) for i in range(2)]
            rsd = [sb("rsdD%d" % i, [P, 1], F32, st) for i in range(2)]
            ynD = [sb("ynD%d" % i, [P, 512], F32, st) for i in range(2)]
            sgD = [sb("sgD%d" % i, [P, 512], F32, st) for i in range(2)]
            ypD = [sb("ypD%d" % i, [P, 512], BF16, st) for i in range(2)]
            yTs = sb("yTsD", [P, 4, 512], BF16, st)
            pq = [ps("pqD%d" % i, [P, 512], F32, st) for i in range(2)]
            pS = ps("pSD", [P, 4, 512], F32, st)
            pTr = ps("pTrD", [P, 1024], BF16, st)
            pmix = ps("pmixD", [P, 512], F32, st)
            k.dma('sp', lgt[:], dlog, w=['lgt'])
            k.dma('sp', dcs[:], dconst, w=['dcs'])
            k.dma('sp', xie[:], xiexp, w=['xie'])
            k.dma('sp', zex[:], zexp, w=['zex'])
            k.dma('sp', wex[:], wexp, w=['wex'])
            k.dma('sp', ccds[:], ccd, w=['ccds'])
            k.dma('sp', ccvs[:], ccv, w=['ccvs'])
            gF = sb("gFD", [P, 512], F32, st)
            gB = sb("gBD", [P, 512], F32, st)
            c128 = sb("c128D", [P, 1], F32, st)
            k.op('dve', lambda e: e.memset(c128[:], 128.0), w=['c128'])
            k.op('act', lambda e: e.activation(out=lgt[:], in_=lgt[:], func=AF.Sigmoid), r=['lgt'], w=['lgt'])
            k.barrier()
            if os.environ.get('STOPD') == '0':
                st.close(); stB.close(); return nc

            for h in range(4):
                lf = lgt[:, h:h + 1]
                lb = lgt[:, 4 + h:5 + h]
                k.dma('sp', gnr[:], gn_rep[:, h * 512:(h + 1) * 512], w=['gnr'])
                qo, ko, vo, go = 3 * D + h * 256, 4 * D + h * 256, 5 * D + h * 512, 5 * D + 2048 + h * 512
                k.dma('pool', wB[:], w_in[:, ko:ko + 256].rearrange("(kc p) c -> p kc c", p=P), w=['wB'])
                k.dma('pool', wA[:], w_in[:, vo:vo + 512].rearrange("(kc p) c -> p kc c", p=P), w=['wA'])
                def rope_block(tb, dst_fn, dkey):
                    k.dma('sp', cs_[:, 0, :], cos_t[:, tb * 512:(tb + 1) * 512], w=['cs'])
                    k.dma('sp', cs_[:, 1, :], sin_t[:, tb * 512:(tb + 1) * 512], w=['cs'])
                    for half in range(2):
                        k.grp('pe', [(lambda e, kc=kc, half=half: e.matmul(
                            out=pq[half][:], lhsT=wB[:, kc, half * P:(half + 1) * P],
                            rhs=hTo[:, kc, tb * 512:(tb + 1) * 512], start=(kc == 0), stop=(kc == 7)))
                            for kc in range(8)], r=['wB'], w=['pq%d' % half])
                    c_, s_ = cs_[:, 0, :], cs_[:, 1, :]
                    k.op('dve', lambda e: e.tensor_tensor(out=rt[0], in0=pq[0][:], in1=c_, op=ALU.mult), r=['pq0', 'cs'], w=['rt0'])
                    k.op('dve', lambda e: e.tensor_tensor(out=rt[1], in0=pq[1][:], in1=s_, op=ALU.mult), r=['pq1', 'cs'], w=['rt1'])
                    k.op('dve', lambda e: e.tensor_tensor(out=rt[2], in0=pq[0][:], in1=s_, op=ALU.mult), r=['pq0', 'cs'], w=['rt2'])
                    k.op('dve', lambda e: e.tensor_tensor(out=rt[3], in0=pq[1][:], in1=c_, op=ALU.mult), r=['pq1', 'cs'], w=['rt3'])
                    k.op('pool', lambda e: e.tensor_tensor(out=dst_fn(0), in0=rt[0], in1=rt[1], op=ALU.subtract),
                         r=['rt0', 'rt1'], w=[dkey])
                    k.op('pool', lambda e: e.tensor_tensor(out=dst_fn(1), in0=rt[2], in1=rt[3], op=ALU.add),
                         r=['rt2', 'rt3'], w=[dkey])

                for tb in range(8):
                    rope_block(tb, lambda half, tb=tb: kTD[:, half, tb * 512:(tb + 1) * 512], 'kTD')
                for t in range(NT):
                    pb = t % 2
                    k.grp('pe', [(lambda e, kc=kc: e.matmul(out=pq[pb][:], lhsT=hTo[:, kc, t * P:(t + 1) * P],
                                                              rhs=wA[:, kc, :], start=(kc == 0), stop=(kc == 7)))
                                 for kc in range(8)], r=['wA'], w=['pq%d' % pb])
                    k.op('act', lambda e: e.activation(out=vD[:, t, :], in_=pq[pb][:], func=AF.Copy), r=['pq%d' % pb], w=['vD'])
                k.op('dve', lambda e: e.tensor_scalar(out=gF[:], in0=xie[:, 0, :], scalar1=0.0, scalar2=lf, op0=ALU.mult, op1=ALU.add),
                     w=['gF'])
                k.op('dve', lambda e: e.tensor_scalar(out=gB[:], in0=xie[:, 0, :], scalar1=0.0, scalar2=lb, op0=ALU.mult, op1=ALU.add),
                     w=['gB'])

                def ppow(out, base, expo, key):
                    k.op('pool', lambda e: e.tensor_tensor(out=out, in0=base, in1=expo, op=ALU.pow), r=['gF', 'gB'], w=[key])

                ppow(DT[:], gF[:, 0:P], dcs[:, 0, :], 'DT')
                k.op('dve', lambda e: e.scalar_tensor_tensor(out=DT[:], in0=DT[:], scalar=1.0 / 16, in1=dcs[:, 1, :],
                                                             op0=ALU.mult, op1=ALU.mult), r=['DT'], w=['DT'])
                ppow(tmpD[:], gB[:, 0:P], dcs[:, 2, :], 'tmpD')
                k.op('dve', lambda e: e.scalar_tensor_tensor(out=tmpD[:], in0=tmpD[:], scalar=1.0 / 16, in1=dcs[:, 3, :],
                                                             op0=ALU.mult, op1=ALU.mult), r=['tmpD'], w=['tmpD'])
                k.op('dve', lambda e: e.tensor_tensor(out=DT[:], in0=DT[:], in1=tmpD[:], op=ALU.add), r=['DT', 'tmpD'], w=['DT'])
                for d_, gg_ in ((0, gF), (1, gB)):
                    ppow(xiT[:, d_, :], gg_[:], xie[:, d_, :], 'xiT')
                    ppow(zT[:, d_:d_ + 1], gg_[:, 0:1], zex[:, d_:d_ + 1], 'zT')
                    ppow(wT[:, d_, :], gg_[:, 0:NT], wex[:, d_, :], 'wT')
                    ppow(gC[:, d_:d_ + 1], gg_[:, 0:1], c128[:], 'gC')
                    ppow(wcc[:, d_ * 8:(d_ + 1) * 8], gg_[:, 0:8], ccds[:, d_ * 8:(d_ + 1) * 8], 'wcc')
                k.op('dve', lambda e: e.tensor_scalar(out=xiT[:], in0=xiT[:], scalar1=1.0 / 16, scalar2=None, op0=ALU.mult),
                     r=['xiT'], w=['xiT'])
                k.op('dve', lambda e: e.tensor_scalar(out=wT[:], in0=wT[:], scalar1=1e-30, scalar2=None, op0=ALU.max), r=['wT'], w=['wT'])
                k.op('dve', lambda e: e.tensor_scalar(out=wcc[:], in0=wcc[:], scalar1=1e-30, scalar2=None, op0=ALU.max), r=['wcc'], w=['wcc'])
                k.op('dve', lambda e: e.tensor_tensor(out=wcc[:], in0=wcc[:], in1=ccvs[:], op=ALU.mult), r=['wcc'], w=['wcc'])

                if os.environ.get('STOPD') == '1':
                    k.barrier(); st.close(); stB.close(); return nc

                k.barrier()
                if os.environ.get('STOPD') == '2':
                    st.close(); stB.close(); return nc
                if mode != 'pre':
                    k.dma('pool', wB[:], w_in[:, qo:qo + 256].rearrange("(kc p) c -> p kc c", p=P), w=['wB'])
                    k.dma('pool', wA[:], w_in[:, go:go + 512].rearrange("(kc p) c -> p kc c", p=P), w=['wA'])

                def ktrans(c, scf, scb, bsel):
                    k.grp('pe', [(lambda e, dh=dh: e.transpose(out=pTr[:, dh * P:(dh + 1) * P], in_=kTD[:, dh, c * P:(c + 1) * P],
                                                               identity=ident[:])) for dh in range(2)], r=['kTD'], w=['pTrk'])
                    if os.environ.get('KTNOEV'):
                        return
                    if scf is not None:
                        if os.environ.get('KTPLAIN'):
                            k.op('act', lambda e: e.activation(out=ktf[bsel][:], in_=pTr[:, 0:256], func=AF.Copy),
                                 r=['pTrk'], w=['ktf%d' % bsel])
                        else:
                            if scb is not None:
                                k.op('dve', lambda e: e.tensor_scalar(out=ktf[bsel][:], in0=pTr[:, 0:256], scalar1=scf, scalar2=None,
                                                                      op0=ALU.mult), r=['pTrk'], w=['ktf%d' % bsel])
                            else:
                                k.op('act', lambda e: e.activation(out=ktf[bsel][:], in_=pTr[:, 0:256], func=AF.Copy, scale=scf),
                                     r=['pTrk'], w=['ktf%d' % bsel])
                    if scb is not None:
                        if os.environ.get('KTPLAIN'):
                            k.op('dve', lambda e: e.tensor_copy(out=ktb[bsel][:], in_=pTr[:, 0:256]), r=['pTrk'], w=['ktb%d' % bsel])
                        else:
                            k.op('dve', lambda e: e.tensor_scalar(out=ktb[bsel][:], in0=pTr[:, 0:256], scalar1=scb, scalar2=None,
                                                                  op0=ALU.mult), r=['pTrk'], w=['ktb%d' % bsel])

                if mode != 'main':
                    ktrans(0, wT[:, 0, 0:1], wT[:, 1, 0:1], 0)
                    for c in range(NT):
                        bs = c % 2
                        if c + 1 < NT:
                            ktrans(c + 1, wT[:, 0, c + 1:c + 2], wT[:, 1, c + 1:c + 2], 1 - bs)
                        fns = []
                        for dh in range(2):
                            fns.append(lambda e, dh=dh: e.matmul(out=pS[:, dh, :], lhsT=ktf[bs][:, dh * P:(dh + 1) * P], rhs=vD[:, c, :],
                                                                 start=(c == 0), stop=(c == NT - 1)))
                            fns.append(lambda e, dh=dh: e.matmul(out=pS[:, 2 + dh, :], lhsT=ktb[bs][:, dh * P:(dh + 1) * P], rhs=vD[:, c, :],
                                                                 start=(c == 0), stop=(c == NT - 1)))
                        if os.environ.get('D2SKIP') != 'mm':
                            k.grp('pe', fns, r=['ktf%d' % bs, 'ktb%d' % bs], w=['pS'])
                    for q4 in range(4 if os.environ.get('D2SKIP') != 'mm' else 0):
                        if q4 % 2 == 0:
                            k.op('act', lambda e, q4=q4: e.activation(out=sloc[:, q4 * 512:(q4 + 1) * 512], in_=pS[:, q4, :], func=AF.Copy),
                                 r=['pS'], w=['sloc'])
                        else:
                            k.op('dve', lambda e, q4=q4: e.tensor_copy(out=sloc[:, q4 * 512:(q4 + 1) * 512], in_=pS[:, q4, :]),
                                 r=['pS'], w=['sloc'])
                if mode == 'pre':
                    k.dma('sp', sloc_out[h], sloc, r=['sloc'])
                    k.barrier()
                    if os.environ.get('STOPD') == '3':
                        st.close(); stB.close(); return nc
                    continue
                if mode == 'fused':
                    k.dma('pool', sloc_h[h].ap(), sloc, r=['sloc'], w=['sloc_d'])
                    k._wait('pool', k._deps(['sloc_d'], []))
                    nc.gpsimd.collective_compute("AllGather", ALU.bypass, replica_groups=[[0, 1, 2, 3], [4, 5, 6, 7]],
                                                 ins=[sloc_h[h].ap().opt()], outs=[gath_h[h].ap().opt()]).then_inc(cc_sem, 1)
                    nc.gpsimd.wait_ge(cc_sem, h + 1)
                    k.op('pool', lambda e: e.memset(mark[:], 0.0), w=['gath_d'])
                    gv = gath_h[h].ap()
                else:
                    gv = gath_in[h]
                for i in range(4):
                    k.dma('sp', gbuf, gv[i * P:(i + 1) * P, :], r=['gath_d'], w=['gbuf'])
                    for d_ in range(2):
                        src = gbuf[:, d_ * 1024:(d_ + 1) * 1024]
                        wc = wcc[:, d_ * 8 + i:d_ * 8 + i + 1]
                        if i == 0:
                            k.op('dve', lambda e, d_=d_, src=src, wc=wc: e.tensor_scalar(out=Sst[:, d_, :], in0=src, scalar1=wc,
                                                                                       scalar2=None, op0=ALU.mult),
                                 r=['gbuf', 'wcc'], w=['Sst%d' % d_])
                        else:
                            k.op('dve', lambda e, d_=d_, src=src, wc=wc: e.scalar_tensor_tensor(
                                out=Sst[:, d_, :], in0=src, scalar=wc, in1=Sst[:, d_, :], op0=ALU.mult, op1=ALU.add),
                                 r=['gbuf', 'wcc'], w=['Sst%d' % d_])
                k.barrier()
                ktrans(NT - 1, None, zT[:, 1:2], (NT - 1) % 2)
                for c in range(NT - 1, -1, -1):
                    bs = c % 2
                    k.op('act', lambda e: e.activation(out=Sbb[bs][:], in_=Sst[:, 1, :], func=AF.Copy), r=['Sst1'], w=['Sbb%d' % bs])
                    k.dma('sp', sb_d[c], Sbb[bs][:], r=['Sbb%d' % bs], w=[('sb_d', c)])
                    if c > 0:
                        ktrans(c - 1, None, zT[:, 1:2], 1 - bs)
                    pb_ = 2 * bs
                    k.grp('pe', [(lambda e, dh=dh: e.matmul(out=pS[:, pb_ + dh, :], lhsT=ktb[bs][:, dh * P:(dh + 1) * P], rhs=vD[:, c, :],
                                                              start=True, stop=True)) for dh in range(2)],
                          r=['ktb%d' % bs], w=['pSb%d' % bs])
                    for dh in range(2):
                        k.op('dve', lambda e, dh=dh: e.scalar_tensor_tensor(
                            out=Sst[:, 1, dh * 512:(dh + 1) * 512], in0=Sst[:, 1, dh * 512:(dh + 1) * 512], scalar=gC[:, 1:2],
                            in1=pS[:, pb_ + dh, :], op0=ALU.mult, op1=ALU.add), r=['pSb%d' % bs, 'Sst1'], w=['Sst1'])
                k.op('act', lambda e: e.activation(out=Sbf[0][:], in_=Sst[:, 0, :], func=AF.Copy), r=['Sst0'], w=['Sbf0'])
                pend_y = []

                def emit_ytr():
                    while pend_y:
                        cc_, bs_ = pend_y.pop(0)
                        k.grp('pe', [(lambda e, fc=fc: e.transpose(out=pTr[:, 512 + fc * P:512 + (fc + 1) * P],
                                                                   in_=ypD[bs_][:, fc * P:(fc + 1) * P], identity=ident[:]))
                                     for fc in range(4)], r=['ypD%d' % bs_], w=['pTry'])
                        k.op('act', lambda e: e.activation(out=yTs[:, :, cc_ * P:(cc_ + 1) * P],
                                                           in_=pTr[:, 512:1024].rearrange("p (f t) -> p f t", f=4), func=AF.Copy),
                             r=['pTry'], w=['yTs'])

                for tb in range(8):
                    rope_block(tb, lambda half: qTD[:, 0, half, :], 'qTD')
                    for d_ in range(2):
                        for half in range(2):
                            k.op('pool', lambda e, d_=d_, half=half: e.tensor_tensor(
                                out=qTD[:, 1 + d_, half, :], in0=qTD[:, 0, half, :], in1=xiT[:, d_, :], op=ALU.mult),
                                 r=['qTD', 'xiT'], w=['qTD'])
                    for cc in range(4):
                        c = tb * 4 + cc
                        bs = c % 2
                        k.dma('sp', Sbb[bs][:], sb_d[c], r=[('sb_d', c)], w=['Sbb%d' % bs])
                        k.grp('pe', [(lambda e, dh=dh: e.matmul(out=pmix[:, 0:P], lhsT=kTD[:, dh, c * P:(c + 1) * P],
                                                                  rhs=qTD[:, 0, dh, cc * P:(cc + 1) * P],
                                                                  start=(dh == 0), stop=(dh == 1))) for dh in range(2)],
                              r=['qTD'], w=['pmix'])
                        k.op('dve', lambda e: e.tensor_tensor(out=Am[bs][:], in0=pmix[:, 0:P], in1=DT[:], op=ALU.mult),
                             r=['pmix', 'DT'], w=['Am%d' % bs])
                        k.grp('pe', [(lambda e, kc=kc: e.matmul(out=pq[1][:], lhsT=hTo[:, kc, c * P:(c + 1) * P],
                                                                  rhs=wA[:, kc, :], start=(kc == 0), stop=(kc == 7)))
                                     for kc in range(8)], r=['wA'], w=['pq1'])
                        fns = [lambda e: e.matmul(out=pq[0][:], lhsT=Am[bs][:], rhs=vD[:, c, :], start=True, stop=False)]
                        for dh in range(2):
                            fns.append(lambda e, dh=dh: e.matmul(out=pq[0][:], lhsT=qTD[:, 1, dh, cc * P:(cc + 1) * P],
                                                                 rhs=Sbf[bs][:, dh * 512:(dh + 1) * 512], start=False, stop=False))
                        for dh in range(2):
                            fns.append(lambda e, dh=dh: e.matmul(out=pq[0][:], lhsT=qTD[:, 2, dh, cc * P:(cc + 1) * P],
                                                                 rhs=Sbb[bs][:, dh * 512:(dh + 1) * 512], start=False, stop=(dh == 1)))
                        k.grp('pe', fns, r=['Am%d' % bs, 'qTD', 'Sbf%d' % bs, 'Sbb%d' % bs], w=['pq0'])
                        ktrans(c, zT[:, 0:1], None, bs)
                        pb_ = 2 * bs
                        k.grp('pe', [(lambda e, dh=dh: e.matmul(out=pS[:, pb_ + dh, :], lhsT=ktf[bs][:, dh * P:(dh + 1) * P], rhs=vD[:, c, :],
                                                                  start=True, stop=True)) for dh in range(2)],
                              r=['ktf%d' % bs], w=['pSf%d' % bs])
                        emit_ytr()
                        for dh in range(2):
                            k.op('dve', lambda e, dh=dh: e.scalar_tensor_tensor(
                                out=Sst[:, 0, dh * 512:(dh + 1) * 512], in0=Sst[:, 0, dh * 512:(dh + 1) * 512], scalar=gC[:, 0:1],
                                in1=pS[:, pb_ + dh, :], op0=ALU.mult, op1=ALU.add), r=['pSf%d' % bs, 'Sst0'], w=['Sst0'])
                        k.op('act', lambda e: e.activation(out=Sbf[1 - bs][:], in_=Sst[:, 0, :], func=AF.Copy),
                             r=['Sst0'], w=['Sbf%d' % (1 - bs)])
                        k.op('dve', lambda e: e.bn_stats(out=stt[bs][:], in_=pq[0][:]), r=['pq0'], w=['stt%d' % bs])
                        k.op('dve', lambda e: e.bn_aggr(out=mv[bs][:], in_=stt[bs][:]), r=['stt%d' % bs], w=['mv%d' % bs])
                        k.op('dve', lambda e: e.tensor_scalar(out=ynD[bs][:], in0=pq[0][:], scalar1=mv[bs][:, 0:1],
                                                              scalar2=None, op0=ALU.subtract),
                             r=['pq0', 'mv%d' % bs], w=['ynD%d' % bs])
                        rstd_from_ss(mv[bs][:, 1:2], rsd[bs][:], 'mv%d' % bs, 'rsd%d' % bs, 1.0)
                        k.op('act', lambda e: e.activation(out=sgD[bs][:], in_=pq[1][:], func=AF.Silu), r=['pq1'], w=['sgD%d' % bs])
                        k.op('pool', lambda e: e.tensor_tensor(out=ynD[bs][:], in0=ynD[bs][:], in1=gnr[:], op=ALU.mult),
                             r=['ynD%d' % bs, 'gnr'], w=['ynD%d' % bs])
                        k.op('dve', lambda e: e.scalar_tensor_tensor(out=ypD[bs][:], in0=ynD[bs][:], scalar=rsd[bs][:], in1=sgD[bs][:],
                                                                     op0=ALU.mult, op1=ALU.mult),
                             r=['ynD%d' % bs, 'sgD%d' % bs, 'rsd%d' % bs], w=['ypD%d' % bs])
                        pend_y.append((cc, bs))
                    emit_ytr()
                    k.dma('sp', yretT_d[h * 4:(h + 1) * 4, :, tb * 512:(tb + 1) * 512].rearrange("f p t -> p f t"),
                          yTs[:], r=['yTs'])
                k.barrier()
        stB.close()

        if stop == 'D' or mode == 'pre':
            stB.close()
            return nc
        with ExitStack() as st:
            Wun = sb("Wun", [P, 8, D], BF16, st)
            Wur = sb("Wur", [P, 16, D], BF16, st)
            Wgn = sb("Wgn", [P, 8, D], BF16, st)
            Wgr = sb("Wgr", [P, 8, D], BF16, st)
            Wo = sb("Wo", [P, 8, D], BF16, st)
            gg1 = sb("gg1E", [P, D], F32, st)
            ynT = sb("ynTE", [P, 8, 512], BF16, st)
            yrT = sb("yrTE", [P, 16, 512], BF16, st)
            hTt = sb("hTtE", [P, 8, 512], BF16, st)
            sgn = sb("sgnE", [P, 512], F32, st)
            sgr = sb("sgrE", [P, 512], F32, st)
            m1 = sb("m1E", [P, 512], F32, st)
            m2 = sb("m2E", [P, 512], F32, st)
            mT = sb("mTE", [P, 8, 512], BF16, st)
            xt = [sb("xtE%d" % i, [P, D], F32, st) for i in range(2)]
            tt_ = [sb("ttE%d" % i, [P, D], F32, st) for i in range(2)]
            oc_ = [sb("ocE%d" % i, [P, D], F32, st) for i in range(2)]
            x1 = [sb("x1E%d" % i, [P, D], F32, st) for i in range(2)]
            xn2 = [sb("xn2E%d" % i, [P, D], BF16, st) for i in range(2)]
            ssE = [sb("ssE%d" % i, [P, 1], F32, st) for i in range(2)]
            ssh = [sb("sshE%d" % i, [P, 2], F32, st) for i in range(2)]
            rsE = [sb("rsE%d" % i, [P, 1], F32, st) for i in range(2)]
            ss2 = [sb("ss2E%d" % i, [P, 1], F32, st) for i in range(2)]
            rs2 = [sb("rs2E%d" % i, [P, 1], F32, st) for i in range(2)]
            h2s = sb("h2sE", [P, 8, 512], BF16, st)
            pU = [ps("pUE%d" % i, [P, 512], F32, st) for i in range(4)]
            pO = ps("pOE", [P, 2, 512], F32, st)
            pst2 = ps("pst2E", [P, 8, P], BF16, st)
            k.dma('sp', gg1[:], gg_d[0], w=['gg1'])
            for (dst, src, nm) in ((Wun, w_up_na, 'Wun'), (Wur, w_up_ret, 'Wur'),
                                   (Wgn, w_in[:, 9 * D:10 * D], 'Wgn'), (Wgr, w_in[:, 10 * D:11 * D], 'Wgr'),
                                   (Wo, w_out, 'Wo')):
                k.dma('pool', dst[:], src.rearrange("(kc p) c -> p kc c", p=P), w=[nm])

            pendT = []
            for tb in range(8):
                k.dma('sp', ynT[:], ynaT_d[:, :, tb * 512:(tb + 1) * 512].rearrange("f p t -> p f t"), w=['ynT'])
                k.dma('sp', yrT[:], yretT_d[:, :, tb * 512:(tb + 1) * 512].rearrange("f p t -> p f t"), w=['yrT'])
                k.dma('sp', hTt[:], hT_d[:, :, tb * 512:(tb + 1) * 512].rearrange("f p t -> p f t"), w=['hTt'])
                for ob in range(8):
                    osl = slice(ob * P, (ob + 1) * P)
                    k.grp('pe', [(lambda e, fc=fc: e.matmul(out=pU[0][:], lhsT=Wun[:, fc, osl], rhs=ynT[:, fc, :],
                                                              start=(fc == 0), stop=(fc == 7))) for fc in range(8)],
                          r=['Wun', 'ynT'], w=['pU0'])
                    k.grp('pe', [(lambda e, fc=fc: e.matmul(out=pU[1][:], lhsT=Wur[:, fc, osl], rhs=yrT[:, fc, :],
                                                              start=(fc == 0), stop=(fc == 15))) for fc in range(16)],
                          r=['Wur', 'yrT'], w=['pU1'])
                    k.grp('pe', [(lambda e, fc=fc: e.matmul(out=pU[2][:], lhsT=Wgn[:, fc, osl], rhs=hTt[:, fc, :],
                                                              start=(fc == 0), stop=(fc == 7))) for fc in range(8)],
                          r=['Wgn', 'hTt'], w=['pU2'])
                    k.grp('pe', [(lambda e, fc=fc: e.matmul(out=pU[3][:], lhsT=Wgr[:, fc, osl], rhs=hTt[:, fc, :],
                                                              start=(fc == 0), stop=(fc == 7))) for fc in range(8)],
                          r=['Wgr', 'hTt'], w=['pU3'])
                    k.op('act', lambda e: e.activation(out=sgn[:], in_=pU[2][:], func=AF.Sigmoid), r=['pU2'], w=['sgn'])
                    k.op('act', lambda e: e.activation(out=sgr[:], in_=pU[3][:], func=AF.Sigmoid), r=['pU3'], w=['sgr'])
                    k.op('dve', lambda e: e.tensor_tensor(out=m1[:], in0=pU[0][:], in1=sgn[:], op=ALU.mult),
                         r=['pU0', 'sgn'], w=['m1'])
                    k.op('dve', lambda e: e.tensor_tensor(out=m2[:], in0=pU[1][:], in1=sgr[:], op=ALU.mult),
                         r=['pU1', 'sgr'], w=['m2'])
                    k.op('pool', lambda e: e.tensor_tensor(out=mT[:, ob, :], in0=m1[:], in1=m2[:], op=ALU.add),
                         r=['m1', 'm2'], w=['mT'])
                for ts in range(4):
                    t = tb * 4 + ts
                    tb2 = t % 2
                    k.dma('sp', xt[tb2][:], x_ext[HALO + t * P:HALO + (t + 1) * P, :], w=['xt%d' % tb2])
                    for half in range(2):
                        k.grp('pe', [(lambda e, fc=fc, half=half: e.matmul(out=pO[:, half, :], lhsT=mT[:, fc, ts * P:(ts + 1) * P],
                                                                            rhs=Wo[:, fc, half * 512:(half + 1) * 512],
                                                                            start=(fc == 0), stop=(fc == 7))) for fc in range(8)],
                              r=['mT', 'Wo'], w=['pO'])
                    while pendT:
                        pendT.pop(0)()
                    for half in range(2):
                        k.op('act', lambda e, half=half: e.activation(out=tt_[tb2][:, half * 512:(half + 1) * 512], in_=pO[:, half, :],
                                                                      func=AF.Square, accum_out=ssh[tb2][:, half:half + 1]),
                             r=['pO'], w=['tt%d' % tb2, 'ssh%d' % tb2])
                        k.op('act', lambda e, half=half: e.activation(out=oc_[tb2][:, half * 512:(half + 1) * 512], in_=pO[:, half, :],
                                                                      func=AF.Copy), r=['pO'], w=['oc%d' % tb2])
                    k.op('dve', lambda e: e.tensor_tensor(out=ssE[tb2][:], in0=ssh[tb2][:, 0:1], in1=ssh[tb2][:, 1:2], op=ALU.add),
                         r=['ssh%d' % tb2], w=['ssE%d' % tb2])
                    rstd_from_ss(ssE[tb2][:], rsE[tb2][:], 'ssE%d' % tb2, 'rsE%d' % tb2, 1.0 / D)
                    k.op('dve', lambda e: e.scalar_tensor_tensor(
                        out=tt_[tb2][:], in0=oc_[tb2][:], scalar=rsE[tb2][:], in1=gg1[:], op0=ALU.mult, op1=ALU.mult),
                         r=['oc%d' % tb2, 'rsE%d' % tb2, 'gg1'], w=['tt%d' % tb2])
                    k.op('pool', lambda e: e.tensor_tensor(out=x1[tb2][:], in0=tt_[tb2][:], in1=xt[tb2][:], op=ALU.add),
                         r=['tt%d' % tb2, 'xt%d' % tb2], w=['x1%d' % tb2])
                    k.dma('sp', x1_d[t * P:(t + 1) * P, :], x1[tb2][:], r=['x1%d' % tb2])
                    norm_to_T(x1[tb2][:], 'x1%d' % tb2, ss2[tb2], rs2[tb2], xn2[tb2], pst2, 'pst2', gs2, sh2,
                              lambda kc, ts=ts: h2s[:, kc, ts * P:(ts + 1) * P], 'h2s', 'E%d' % tb2, defer=pendT)
                while pendT:
                    pendT.pop(0)()
                k.dma('sp', h2T_d[:, :, tb * 512:(tb + 1) * 512].rearrange("f p t -> p f t"), h2s[:], r=['h2s'])
            k.barrier()
        stB.close()
        if stop == 'Ea':
            return nc

        with ExitStack() as st:
            Wfi = sb("Wfi", [P, 8, 2 * FFH], BF16, st)
            Wfo = sb("Wfo", [P, 22, D], BF16, st)
            gg2 = sb("gg2F", [P, D], F32, st)
            h2t = sb("h2tF", [P, 8, 512], BF16, st)
            sa = [sb("saF%d" % i, [P, 512], F32, st) for i in range(2)]
            hid = sb("hidF", [P, 22, 512], BF16, st)
            x1t = sb("x1tF", [P, D], F32, st)
            tf = sb("tfF", [P, D], F32, st)
            yo = sb("yoF", [P, D], F32, st)
            ssF = [sb("ssF%d" % i, [P, 1], F32, st) for i in range(2)]
            sshF = [sb("sshF%d" % i, [P, 2], F32, st) for i in range(2)]
            rsF = [sb("rsF%d" % i, [P, 1], F32, st) for i in range(2)]
            pA = [ps("pAF%d" % i, [P, 512], F32, st) for i in range(2)]
            pG = [ps("pGF%d" % i, [P, 512], F32, st) for i in range(2)]
            pF = [ps("pFF%d" % i, [P, 2, 512], F32, st) for i in range(2)]
            k.dma('sp', gg2[:], gg_d[1], w=['gg2'])
            for part in range(4):
                k.dma('pool', Wfi[:, :, part * 1408:(part + 1) * 1408],
                      w_ffn_in[:, part * 1408:(part + 1) * 1408].rearrange("(kc p) c -> p kc c", p=P), w=['Wfi'])
            k.dma('pool', Wfo[:], w_ffn_out.rearrange("(fc p) c -> p fc c", p=P), w=['Wfo'])
            for tb in range(8):
                k.dma('sp', h2t[:], h2T_d[:, :, tb * 512:(tb + 1) * 512].rearrange("f p t -> p f t"), w=['h2t'])
                for fb in range(22):
                    f2 = fb % 2
                    k.grp('pe', [(lambda e, kc=kc: e.matmul(out=pA[f2][:], lhsT=Wfi[:, kc, fb * P:(fb + 1) * P], rhs=h2t[:, kc, :],
                                                              start=(kc == 0), stop=(kc == 7))) for kc in range(8)],
                          r=['Wfi', 'h2t'], w=['pA%d' % f2])
                    k.grp('pe', [(lambda e, kc=kc: e.matmul(out=pG[f2][:], lhsT=Wfi[:, kc, FFH + fb * P:FFH + (fb + 1) * P],
                                                              rhs=h2t[:, kc, :], start=(kc == 0), stop=(kc == 7))) for kc in range(8)],
                          r=['Wfi', 'h2t'], w=['pG%d' % f2])
                    k.op('act', lambda e: e.activation(out=sa[f2][:], in_=pA[f2][:], func=AF.Silu), r=['pA%d' % f2], w=['sa%d' % f2])
                    k.op('dve', lambda e: e.tensor_tensor(out=hid[:, fb, :], in0=pG[f2][:], in1=sa[f2][:], op=ALU.mult),
                         r=['pG%d' % f2, 'sa%d' % f2], w=['hid'])
                for ts in range(4):
                    t = tb * 4 + ts
                    t2 = t % 2
                    k.dma('sp', x1t[:], x1_d[t * P:(t + 1) * P, :], w=['x1t'])
                    for half in range(2):
                        k.grp('pe', [(lambda e, fc=fc, half=half: e.matmul(out=pF[t2][:, half, :], lhsT=hid[:, fc, ts * P:(ts + 1) * P],
                                                                            rhs=Wfo[:, fc, half * 512:(half + 1) * 512],
                                                                            start=(fc == 0), stop=(fc == 21))) for fc in range(22)],
                              r=['hid', 'Wfo'], w=['pF%d' % t2])
                    for half in range(2):
                        k.op('act', lambda e, half=half: e.activation(out=tf[:, half * 512:(half + 1) * 512], in_=pF[t2][:, half, :],
                                                                      func=AF.Square, accum_out=sshF[t2][:, half:half + 1]),
                             r=['pF%d' % t2], w=['tf', 'sshF%d' % t2])
                    k.op('dve', lambda e: e.tensor_tensor(out=ssF[t2][:], in0=sshF[t2][:, 0:1], in1=sshF[t2][:, 1:2], op=ALU.add),
                         r=['sshF%d' % t2], w=['ssF%d' % t2])
                    rstd_from_ss(ssF[t2][:], rsF[t2][:], 'ssF%d' % t2, 'rsF%d' % t2, 1.0 / D)
                    for half in range(2):
                        k.op('dve', lambda e, half=half: e.scalar_tensor_tensor(
                            out=tf[:, half * 512:(half + 1) * 512], in0=pF[t2][:, half, :], scalar=rsF[t2][:],
                            in1=gg2[:, half * 512:(half + 1) * 512], op0=ALU.mult, op1=ALU.mult),
                             r=['pF%d' % t2, 'rsF%d' % t2, 'gg2'], w=['tf'])
                    k.op('pool', lambda e: e.tensor_tensor(out=yo[:], in0=tf[:], in1=x1t[:], op=ALU.add),
                         r=['tf', 'x1t'], w=['yo'])
                    k.dma('sp', y_out[t * P:(t + 1) * P, :], yo[:], r=['yo'])
            k.barrier()
    return nc


_NC_CACHE = {}


def _host_tables():
    ki = np.arange(P) // 64
    kj = np.arange(P) % 64
    j = np.arange(P)[:, None].astype(np.float32)
    i = np.arange(P)[None, :].astype(np.float32)
    dconst = np.stack([np.maximum(i - j, 0), (i >= j).astype(np.float32), np.maximum(j - i, 0), (j > i).astype(np.float32)],
                      axis=1).astype(np.float32)
    i512 = (np.arange(512) % P).astype(np.float32)
    xiexp = np.broadcast_to(np.stack([i512 + 1, P - i512], 0)[None], (P, 2, 512)).astype(np.float32).copy()
    jj = np.arange(P).astype(np.float32)
    zexp = np.stack([127 - jj, jj], 1).astype(np.float32)
    c = np.arange(NT).astype(np.float32)[None, :]
    wexp = np.stack([4095 - (128 * c + jj[:, None]), 128 * c + jj[:, None]], 1).astype(np.float32)
    return dconst, xiexp, zexp, wexp


def make_in_maps(x_prompt, x_sample, c_prompt, c_sample, w_mod, b_mod, g_pre_mix, w_in, rpb, ret_decay_logit, ret_gn,
           w_up_na, w_up_ret, w_out, g_post_mix, g_pre_ffn, w_ffn_in, w_ffn_out, g_post_ffn):
    f32 = np.float32
    xs = [np.asarray(x_prompt[0], f32), np.asarray(x_prompt[1], f32), np.asarray(x_sample[0], f32)]
    cs = [np.asarray(c_prompt[0], f32), np.asarray(c_prompt[1], f32), np.asarray(c_sample[0], f32)]
    core_seq = [0, 0, 1, 1, 2, 2, 2, 2]
    core_pos = [0, 1, 0, 1, 0, 1, 2, 3]
    seq_cores = [2, 2, 4]
    rpb0 = np.asarray(rpb[0], f32)

    kidx = np.arange(P)
    ki, kj = kidx // 64, kidx % 64
    qi, qc = kidx // 64, kidx % 64
    bias7 = np.zeros((16, P, 7, P), f32)
    for d in range(-3, 4):
        rr = 2 * d + ki[:, None] - qi[None, :] + 7
        cr = kj[:, None] - qc[None, :] + 15
        ok = (rr >= 0) & (rr < 15) & (cr >= 0) & (cr < 31)
        g = rpb0[:, np.clip(rr, 0, 14), np.clip(cr, 0, 30)]
        bias7[:, :, d + 3, :] = np.where(ok[None], g, 0.0)
    dconst, xiexp, zexp, wexp = _host_tables()
    ident = np.eye(P, dtype=f32).astype(ml_dtypes.bfloat16)
    inv = (1.0 / (np.float32(10000.0) ** (np.arange(128, dtype=f32) / np.float32(128)))).astype(f32)

    in_maps = []
    for core in range(NCORES):
        s, pos, ncs = core_seq[core], core_pos[core], seq_cores[core_seq[core]]
        xseq = xs[s]
        t0 = pos * TOK
        x_ext = np.zeros((EXT, D), f32)
        lo, hi = t0 - HALO, t0 + TOK + HALO
        slo, shi = max(lo, 0), min(hi, xseq.shape[0])
        x_ext[slo - lo:shi - lo] = xseq[slo:shi]
        rows_total = xseq.shape[0] // 64
        r0 = t0 // 64
        comb_in = np.full((16, P, 5, 6, P), NEG, f32)
        for cl, prep in ((0, 0), (1, 1), (2, 2), (3, NT - 2), (4, NT - 1)):
            for bi, dlt in enumerate(NA_CLASS_DELTAS[cl]):
                qrow = r0 + 2 * prep + qi[None, :]
                krow = r0 + 2 * (prep + dlt) + ki[:, None]
                rs = np.clip(qrow - 4, 0, rows_total - 8)
                cst = np.clip(qc[None, :] - 8, 0, 48)
                ok = (krow >= 0) & (krow < rows_total) & (krow >= rs) & (krow < rs + 8) & \
                     (kj[:, None] >= cst) & (kj[:, None] < cst + 16)
                comb_in[:, :, cl, bi, :] = np.where(ok[None], bias7[:, :, dlt + 3, :], NEG)
        posv = (t0 + np.arange(TOK)).astype(f32)
        ang = (posv[None, :] * inv[:, None]).astype(f32)
        ccd = np.zeros(16, f32)
        ccv = np.zeros(16, f32)
        base = 0 if core < 4 else 4
        for sl_ in range(4):
            o = base + sl_
            if core_seq[o] == s:
                if o < core:
                    ccd[sl_] = 4096.0 * (core - 1 - o); ccv[sl_] = 1.0
                if o > core:
                    ccd[8 + sl_] = 4096.0 * (o - core - 1); ccv[8 + sl_] = 1.0
        in_maps.append({
            "x_ext": x_ext,
            "cT": np.ascontiguousarray(cs[s].reshape(8, P).T),
            "w_mod": np.asarray(w_mod[0], f32), "b_mod": np.asarray(b_mod, f32).reshape(1, -1),
            "g_rows": np.concatenate([np.asarray(g_pre_mix[0], f32), np.asarray(g_post_mix[0], f32),
                                      np.asarray(g_pre_ffn[0], f32), np.asarray(g_post_ffn[0], f32)]).reshape(1, -1),
            "w_in": np.asarray(w_in[0], f32), "comb_in": comb_in,
            "dlog": np.ascontiguousarray(np.broadcast_to(np.asarray(ret_decay_logit[0], f32).reshape(1, 8), (P, 8))),
            "gn_rep": np.ascontiguousarray(np.broadcast_to(np.asarray(ret_gn[0], f32).reshape(1, -1), (P, 2048))),
            "w_up_na": np.asarray(w_up_na[0], f32), "w_up_ret": np.asarray(w_up_ret[0], f32),
            "w_out": np.asarray(w_out[0], f32), "w_ffn_in": np.asarray(w_ffn_in[0], f32),
            "w_ffn_out": np.asarray(w_ffn_out[0], f32),
            "cos_t": np.cos(ang).astype(f32), "sin_t": np.sin(ang).astype(f32),
            "ident": ident, "dconst": dconst, "xiexp": xiexp, "zexp": zexp, "wexp": wexp,
            "ccd": np.ascontiguousarray(np.broadcast_to(ccd[None], (P, 16))),
            "ccv": np.ascontiguousarray(np.broadcast_to(ccv[None], (P, 16))),
        })
    return in_maps


def kernel(**inputs):
    in_maps = make_in_maps(**inputs)
    if "fused" not in _NC_CACHE:
        _NC_CACHE["fused"] = build(mode='fused')
    res = run_bass_kernel_spmd(_NC_CACHE["fused"], in_maps, core_ids=list(range(NCORES)))
    outs = [np.asarray(r["y_out"], np.float32) for r in res.results]
    y_prompt = np.stack([np.concatenate(outs[0:2], 0), np.concatenate(outs[2:4], 0)], 0)
    y_sample = np.concatenate(outs[4:8], 0)[None]
    return (y_prompt, y_sample)
```

```python
from contextlib import ExitStack
import os
import numpy as np
import ml_dtypes
import concourse.bass as bass
import concourse.mybir as mybir
from concourse.bass_utils import run_bass_kernel_spmd

F32 = mybir.dt.float32
BF16 = mybir.dt.bfloat16
AF = mybir.ActivationFunctionType
ALU = mybir.AluOpType

P = 128
D = 1024
TOK = 4096
HALO = 256
EXT = TOK + 2 * HALO
NT = TOK // P
NTE = EXT // P
INW = 11264
FFH = 2816
EPS = 1e-6
NEG = -30000.0
NCORES = 8

NA_CLASS_DELTAS = {0: [-2, -1, 0, 1, 2, 3], 1: [-2, -1, 0, 1, 2], 2: [-2, -1, 0, 1, 2],
                   3: [-2, -1, 0, 1, 2], 4: [-3, -2, -1, 0, 1, 2]}


def na_class(p):
    return {0: 0, 1: 1, NT - 2: 3, NT - 1: 4}.get(p, 2)


class Emit:
    def __init__(self, nc, es):
        self.nc = nc
        self.E = {'pe': nc.tensor, 'act': nc.scalar, 'dve': nc.vector, 'pool': nc.gpsimd, 'sp': nc.sync}
        self.sem = {e: es.enter_context(nc.semaphore("sem_" + e)) for e in self.E}
        self.cnt = {e: 0 for e in self.E}
        self.seen = {e: {} for e in self.E}
        self.ND = 56
        self.dsem = [es.enter_context(nc.semaphore("dsem%d" % i)) for i in range(self.ND)]
        self.dval = [0] * self.ND
        self.dpool = {'sp': list(range(0, 36)), 'pool': list(range(36, 56))}
        self.dnext = {'sp': 0, 'pool': 0}
        self.W = {}
        self.R = {}

    def _semobj(self, key):
        return self.sem[key] if isinstance(key, str) else self.dsem[key]

    def _wait(self, e, deps):
        best = {}
        for (k, v) in deps:
            if k == 'pe' and e == 'pe':
                continue
            if v > best.get(k, 0):
                best[k] = v
        for k, v in best.items():
            if self.seen[e].get(k, 0) < v:
                self.E[e].wait_ge(self._semobj(k), v)
                self.seen[e][k] = v

    def _deps(self, r, w, e=None):
        deps = []
        for k in r:
            if k in self.W:
                deps.append(self.W[k])
        for k in w:
            if k in self.W and self.W[k][0] != e:
                deps.append(self.W[k])
            deps += [d for d in self.R.get(k, []) if d[0] != e]
        return deps

    def _record(self, me, r, w):
        for k in r:
            self.R.setdefault(k, []).append(me)
        for k in w:
            self.W[k] = me
            self.R[k] = []

    def grp(self, e, fns, r=(), w=()):
        self._wait(e, self._deps(r, w, e if e in ('act', 'dve', 'pool', 'pe') else None))
        eng = self.E[e]
        ins = None
        for fn in fns:
            ins = fn(eng)
        self.cnt[e] += 1
        ins.then_inc(self.sem[e], 1)
        self._record((e, self.cnt[e]), r, w)

    def op(self, e, fn, r=(), w=()):
        self.grp(e, [fn], r, w)

    def dma(self, q, out, in_, r=(), w=(), **kw):
        self._wait(q, self._deps(r, w))
        pool_ = self.dpool[q]
        i = pool_[self.dnext[q]]
        self.dnext[q] = (self.dnext[q] + 1) % len(pool_)
        self.dval[i] += 16
        self.E[q].dma_start(out=out, in_=in_, **kw).then_inc(self.dsem[i], 16)
        self._record((i, self.dval[i]), r, w)

    def barrier(self):
        deps = [(e, self.cnt[e]) for e in self.E if self.cnt[e] > 0]
        deps += [(i, self.dval[i]) for i in range(self.ND) if self.dval[i] > 0]
        for e in self.E:
            self._wait_all(e, deps)
        self.W = {}
        self.R = {}

    def _wait_all(self, e, deps):
        for (k, v) in deps:
            if k == e:
                continue
            if self.seen[e].get(k, 0) < v:
                self.E[e].wait_ge(self._semobj(k), v)
                self.seen[e][k] = v


def build(stop=None, mode='main'):
    nc = bass.Bass('TRN2', target_bir_lowering=False)
    SK = dict(kind="ExternalOutput") if stop else {}

    def din(name, shape, dt=F32):
        return nc.dram_tensor(name, list(shape), dt, kind="ExternalInput").ap()

    x_ext = din("x_ext", [EXT, D])
    cT = din("cT", [P, 8])
    w_mod = din("w_mod", [D, 6 * D])
    b_mod = din("b_mod", [1, 6 * D])
    g_rows = din("g_rows", [1, 4 * D])
    w_in = din("w_in", [D, INW])
    comb_in = din("comb_in", [16, P, 5, 6, P])
    dlog = din("dlog", [P, 8])
    gn_rep = din("gn_rep", [P, 2048])
    w_up_na = din("w_up_na", [D, D])
    w_up_ret = din("w_up_ret", [2048, D])
    w_out = din("w_out", [D, D])
    w_ffn_in = din("w_ffn_in", [D, 2 * FFH])
    w_ffn_out = din("w_ffn_out", [FFH, D])
    cos_t = din("cos_t", [P, TOK])
    sin_t = din("sin_t", [P, TOK])
    ident_in = din("ident", [P, P], BF16)
    dconst = din("dconst", [P, 4, P])
    xiexp = din("xiexp", [P, 2, 512])
    zexp = din("zexp", [P, 2])
    wexp = din("wexp", [P, 2, NT])
    ccd = din("ccd", [P, 16])
    ccv = din("ccv", [P, 16])
    y_out = nc.dram_tensor("y_out", [TOK, D], F32, kind="ExternalOutput").ap()
    if mode == 'pre':
        sloc_out = nc.dram_tensor("sloc_out", [4, P, 2048], F32, kind="ExternalOutput").ap()
    if mode == 'main':
        gath_in = din("gath_in", [4, 4 * P, 2048])

    hT_d = nc.dram_tensor("hT_d", [8, P, TOK], BF16, **(SK if stop == "B" else {})).ap()
    ynaT_d = nc.dram_tensor("ynaT_d", [8, P, TOK], BF16, **(SK if stop == "C" else {})).ap()
    yretT_d = nc.dram_tensor("yretT_d", [16, P, TOK], BF16, **(SK if stop == "D" else {})).ap()
    sb_d = nc.dram_tensor("sb_d", [NT, P, 1024], BF16).ap()
    sloc_h = [nc.dram_tensor("sloc_d%d" % h, [P, 2048], F32) for h in range(4)]
    gath_h = [nc.dram_tensor("gath_d%d" % h, [4 * P, 2048], F32) for h in range(4)]
    x1_d = nc.dram_tensor("x1_d", [TOK, D], F32, **(SK if stop == "Ea" else {})).ap()
    h2T_d = nc.dram_tensor("h2T_d", [8, P, TOK], BF16, **(SK if stop == "Ea" else {})).ap()
    gg_d = nc.dram_tensor("gg_d", [2, P, D], F32, **(SK if stop == "A" else {})).ap()

    es = ExitStack()
    with es:
        k = Emit(nc, es)
        cc_sem = es.enter_context(nc.semaphore("cc_sem"))

        def sb(name, shape, dt=F32, stack=es):
            return stack.enter_context(nc.sbuf_tensor(name, list(shape), dt))

        def ps(name, shape, dt=F32, stack=es):
            return stack.enter_context(nc.psum_tensor(name, list(shape), dt))

        ident = sb("ident_sb", [P, P], BF16)
        ones_r = sb("ones_r", [1, P], F32)
        gs1 = sb("gs1", [P, 8]); sh1 = sb("sh1", [P, 8])
        gs2 = sb("gs2", [P, 8]); sh2 = sb("sh2", [P, 8])
        nhalf = sb("nhalf", [P, 1])
        k.dma('sp', ident[:], ident_in, w=['ident'])
        k.op('dve', lambda e: e.memset(ones_r[:], 1.0), w=['ones_r'])
        k.op('dve', lambda e: e.memset(nhalf[:], -0.5), w=['nhalf'])

        def rstd_from_ss(ss, rstd, key_ss, key_rstd, scale):
            k.op('dve', lambda e: e.tensor_scalar(out=rstd, in0=ss, scalar1=scale, scalar2=EPS,
                                                  op0=ALU.mult, op1=ALU.add), r=[key_ss], w=[key_rstd])
            k.op('pool', lambda e: e.tensor_tensor(out=rstd, in0=rstd, in1=nhalf[:], op=ALU.pow),
                 r=[key_rstd, 'nhalf'], w=[key_rstd])

        with ExitStack() as st:
            csb = sb("csb", [P, 8], F32, st)
            sil = sb("sil", [P, 8], F32, st)
            wm = [sb("wm%d" % i, [P, 8, 512], F32, st) for i in range(2)]
            brow = sb("brow", [1, 6 * D], F32, st)
            grow = sb("grow", [1, 4 * D], F32, st)
            mrow = sb("mrow", [1, 6 * D], F32, st)
            r1 = sb("r1", [P, 4 * D], F32, st)
            E0 = sb("E0", [P, P], F32, st)
            e0 = sb("e0c", [P, 1], F32, st)
            k.op('dve', lambda e: e.memset(r1[:], 0.0), w=['r1'])
            k.op('dve', lambda e: e.memset(E0[:], 0.0), w=['E0'])
            k.op('dve', lambda e: e.memset(E0[0:1, :], 1.0), w=['E0'])
            k.op('dve', lambda e: e.memset(e0[:], 0.0), w=['E0'])
            k.op('dve', lambda e: e.memset(e0[0:1, :], 1.0), w=['E0'])
            pm = [ps("pm%d" % i, [1, 512], F32, st) for i in range(2)]
            pT = ps("pT", [P, 32], F32, st)
            pB = [ps("pB%d" % i, [P, 512], F32, st) for i in range(2)]
            gg1 = sb("gg1", [P, D], F32, st); gg2 = sb("gg2", [P, D], F32, st)
            k.dma('sp', csb[:], cT, w=['csb'])
            k.dma('sp', brow[:], b_mod, w=['brow'])
            k.dma('sp', grow[:], g_rows, w=['grow'])
            k.op('act', lambda e: e.activation(out=sil[:], in_=csb[:], func=AF.Silu), r=['csb'], w=['sil'])
            if os.environ.get('STOPA') == '0':
                k.barrier(); return nc
            for g in range(12):
                b = g % 2
                k.dma('sp', wm[b][:], w_mod[:, g * 512:(g + 1) * 512].rearrange("(kc p) c -> p kc c", p=P),
                      w=['wm%d' % b])
                k.grp('pe', [(lambda e, kc=kc: e.matmul(out=pm[b][:], lhsT=sil[:, kc:kc + 1], rhs=wm[b][:, kc, :],
                                                          start=(kc == 0), stop=(kc == 7))) for kc in range(8)],
                      r=['sil', 'wm%d' % b], w=['pm%d' % b])
                k.op('dve', lambda e: e.tensor_tensor(out=mrow[:, g * 512:(g + 1) * 512], in0=pm[b][:],
                                                      in1=brow[:, g * 512:(g + 1) * 512], op=ALU.add),
                     r=['pm%d' % b, 'brow'], w=['mrow'])
            if os.environ.get('STOPA') == '1':
                k.barrier(); return nc
            k.op('dve', lambda e: e.scalar_tensor_tensor(out=r1[0:1, 0:D], in0=mrow[:, D:2 * D], scalar=1.0,
                                                         in1=grow[:, 0:D], op0=ALU.add, op1=ALU.mult),
                 r=['mrow', 'grow'], w=['r1'])
            k.op('dve', lambda e: e.tensor_copy(out=r1[0:1, D:2 * D], in_=mrow[:, 0:D]), r=['mrow'], w=['r1'])
            k.op('dve', lambda e: e.scalar_tensor_tensor(out=r1[0:1, 2 * D:3 * D], in0=mrow[:, 4 * D:5 * D], scalar=1.0,
                                                         in1=grow[:, 2 * D:3 * D], op0=ALU.add, op1=ALU.mult),
                 r=['mrow', 'grow'], w=['r1'])
            k.op('dve', lambda e: e.tensor_copy(out=r1[0:1, 3 * D:4 * D], in_=mrow[:, 3 * D:4 * D]), r=['mrow'], w=['r1'])
            if os.environ.get('STOPA') == '2':
                k.barrier(); return nc
            for j in range(32):
                k.op('pe', lambda e, j=j: e.matmul(out=pT[:, j:j + 1], lhsT=r1[:, j * P:(j + 1) * P],
                                                   rhs=e0[:, 0:1], start=True, stop=True),
                     r=['r1', 'E0'], w=['pT'])
            for i, dst in enumerate([gs1, sh1, gs2, sh2]):
                k.op('dve', lambda e, i=i, dst=dst: e.tensor_copy(out=dst[:], in_=pT[:, i * 8:(i + 1) * 8]),
                     r=['pT'], w=['gsv%d' % i])
            if os.environ.get('STOPA') == '3':
                k.barrier(); return nc
            k.op('dve', lambda e: e.tensor_tensor(out=r1[0:1, 0:D], in0=mrow[:, 2 * D:3 * D], in1=grow[:, D:2 * D],
                                                  op=ALU.mult), r=['mrow', 'grow', 'r1'], w=['r1'])
            k.op('dve', lambda e: e.tensor_tensor(out=r1[0:1, D:2 * D], in0=mrow[:, 5 * D:6 * D], in1=grow[:, 3 * D:4 * D],
                                                  op=ALU.mult), r=['mrow', 'grow'], w=['r1'])
            for i, dst in enumerate([gg1, gg2]):
                for hh in range(2):
                    k.op('pe', lambda e, i=i, hh=hh: e.matmul(out=pB[hh][:], lhsT=E0[:, :],
                                                              rhs=r1[:, i * D + hh * 512:i * D + (hh + 1) * 512],
                                                              start=True, stop=True),
                         r=['r1', 'E0'], w=['pB%d' % hh])
                    k.op('dve', lambda e, dst=dst, hh=hh: e.tensor_copy(out=dst[:, hh * 512:(hh + 1) * 512], in_=pB[hh][:]),
                         r=['pB%d' % hh], w=['gg'])
            k.dma('sp', gg_d[0], gg1[:], r=['gg'])
            k.dma('sp', gg_d[1], gg2[:], r=['gg'])
            if stop:
                dbgA = nc.dram_tensor("dbgA", [P, 32], F32, kind="ExternalOutput").ap()
                for i, src_ in enumerate([gs1, sh1, gs2, sh2]):
                    k.dma('sp', dbgA[:, i * 8:(i + 1) * 8], src_[:], r=['gsv%d' % i])
            k.barrier()
            if stop == 'A':
                return nc

        def norm_to_T(xt, key_x, ssb, rsb, xnb, pst, key_ps, gsv, shv, dst_fn, key_dst, tag, defer=None):
            k.op('act', lambda e: e.activation(out=xnb[:], in_=xt, func=AF.Square, accum_out=ssb[:]),
                 r=[key_x], w=['xn' + tag, 'ss' + tag])
            rstd_from_ss(ssb[:], rsb[:], 'ss' + tag, 'rs' + tag, 1.0 / D)
            k.op('act', lambda e: e.activation(out=xnb[:], in_=xt, func=AF.Copy, scale=rsb[:]),
                 r=[key_x, 'rs' + tag], w=['xn' + tag])
            def part2():
                k.grp('pe', [(lambda e, kc=kc: e.transpose(out=pst[:, kc, :], in_=xnb[:, kc * P:(kc + 1) * P], identity=ident[:]))
                             for kc in range(8)], r=['xn' + tag], w=[key_ps])
                for kc in range(8):
                    k.op('dve', lambda e, kc=kc: e.tensor_scalar(out=dst_fn(kc), in0=pst[:, kc, :], scalar1=gsv[:, kc:kc + 1],
                                                                 scalar2=shv[:, kc:kc + 1], op0=ALU.mult, op1=ALU.add),
                         r=[key_ps], w=[key_dst])
            if defer is None:
                part2()
            else:
                defer.append(part2)

        stB = ExitStack()
        hTo = sb("hTo", [P, 8, TOK], BF16, stB)
        stH = ExitStack()
        hTh = sb("hTh", [P, 8, 512], BF16, stH)

        def mt(t):
            return t - 2 if 2 <= t < 2 + NT else (t + 32 if t < 2 else t)

        def hsrc(t):
            m = mt(t)
            return (hTo, m) if m < NT else (hTh, m - NT)

        with ExitStack() as st:
            xb = [sb("xb%d" % i, [P, D], F32, st) for i in range(3)]
            xn = [sb("xnB%d" % i, [P, D], BF16, st) for i in range(2)]
            ssB = [sb("ssB%d" % i, [P, 1], F32, st) for i in range(2)]
            rsB = [sb("rsB%d" % i, [P, 1], F32, st) for i in range(2)]
            pst = [ps("pstB%d" % i, [P, 8, P], BF16, st) for i in range(2)]
            def b_sq(t):
                b3, b2 = t % 3, t % 2
                k.dma('sp', xb[b3][:], x_ext[t * P:(t + 1) * P, :], w=['xb%d' % b3])
                k.op('act', lambda e: e.activation(out=xsq[:], in_=xb[b3][:], func=AF.Square, accum_out=ssB[b2][:]),
                     r=['xb%d' % b3], w=['xsq', 'ssB%d' % b2])
                rstd_from_ss(ssB[b2][:], rsB[b2][:], 'ssB%d' % b2, 'rsB%d' % b2, 1.0 / D)

            def b_rest(t):
                b3, b2 = t % 3, t % 2
                hb_, hm_ = hsrc(t)
                k.op('act', lambda e: e.activation(out=xn[b2][:], in_=xb[b3][:], func=AF.Copy, scale=rsB[b2][:]),
                     r=['xb%d' % b3, 'rsB%d' % b2], w=['xnB%d' % b2])
                k.grp('pe', [(lambda e, kc=kc: e.transpose(out=pst[b2][:, kc, :], in_=xn[b2][:, kc * P:(kc + 1) * P], identity=ident[:]))
                             for kc in range(8)], r=['xnB%d' % b2], w=['pstB%d' % b2])
                for kc in range(8):
                    k.op('dve', lambda e, kc=kc: e.tensor_scalar(out=hb_[:, kc, hm_ * P:(hm_ + 1) * P], in0=pst[b2][:, kc, :],
                                                                 scalar1=gs1[:, kc:kc + 1], scalar2=sh1[:, kc:kc + 1],
                                                                 op0=ALU.mult, op1=ALU.add),
                         r=['pstB%d' % b2], w=[('hT', t)])
                if 2 <= t < 2 + NT and (t - 2) % 4 == 3:
                    o0 = t - 2 - 3
                    k.dma('sp', hT_d[:, :, o0 * P:(o0 + 4) * P].rearrange("kc p t -> p kc t"),
                          hTo[:, :, o0 * P:(o0 + 4) * P], r=[('hT', tt + 2) for tt in range(o0, o0 + 4)])

            xsq = sb("xsqB", [P, D], BF16, st)
            b_sq(0)
            for t in range(NTE):
                if t + 1 < NTE:
                    b_sq(t + 1)
                b_rest(t)
            k.barrier()
        if stop == 'B':
            stH.close(); stB.close()
            return nc

        with ExitStack() as st:
            wq = [sb("wqC%d" % i, [P, 8, P], BF16, st) for i in range(2)]
            wk = [sb("wkC%d" % i, [P, 8, P], BF16, st) for i in range(2)]
            wv = [sb("wvC%d" % i, [P, 8, P], BF16, st) for i in range(2)]
            comb = [sb("combC%d" % i, [P, 2, 5, 6, P], F32, st) for i in range(2)]
            qT = sb("qTC", [P, TOK], BF16, st)
            kT = sb("kTC", [P, EXT], BF16, st)
            vS = sb("vSC", [P, NTE, 2, 65], BF16, st)
            tS = [sb("tSC%d" % i, [P, 6, P], F32, st) for i in range(2)]
            pT_ = [sb("pTC%d" % i, [P, 6, P], BF16, st) for i in range(2)]
            rinv = [sb("rinvC%d" % i, [P, 2], F32, st) for i in range(2)]
            yt = [sb("ytC%d" % i, [P, P], BF16, st) for i in range(2)]
            yT = [sb("yTC%d" % i, [P, TOK], BF16, st) for i in range(2)]
            pproj = [ps("pprojC%d" % i, [P, 512], F32, st) for i in range(2)]
            psA = [ps("psAC%d" % i, [P, 4, P], F32, st) for i in range(2)]
            psB = [ps("psBC%d" % i, [P, 2, P], F32, st) for i in range(2)]
            po = ps("poC", [P, 2, 2, 65], F32, st)
            pyt = ps("pytC", [P, 2, P], BF16, st)
            k.op('pool', lambda e: e.memset(vS[:, :, :, 64:65], 1.0), w=['vS'])

            def load_w(hp):
                b = hp % 2
                for (dst, off, nm) in ((wq, 0, 'wq'), (wk, D, 'wk'), (wv, 2 * D, 'wv')):
                    k.dma('pool', dst[b][:], w_in[:, off + hp * P: off + (hp + 1) * P].rearrange("(kc p) c -> p kc c", p=P),
                          w=['%s%d' % (nm, b)])
                k.dma('sp', comb[b][:], comb_in[2 * hp:2 * hp + 2].rearrange("h p c b q -> p h c b q"), w=['comb%d' % b])

            if mode != 'pre':
                load_w(0)
            for hp in range(8 if mode != 'pre' else 0):
                b = hp % 2
                if hp + 1 < 8:
                    load_w(hp + 1)
                for tb in range(8):
                    pb = tb % 2
                    k.grp('pe', [(lambda e, kc=kc: e.matmul(out=pproj[pb][:], lhsT=wq[b][:, kc, :],
                                                              rhs=hTo[:, kc, tb * 512:(tb + 1) * 512],
                                                              start=(kc == 0), stop=(kc == 7))) for kc in range(8)],
                          r=['wq%d' % b], w=['pproj%d' % pb])
                    k.op('act', lambda e: e.activation(out=qT[:, tb * 512:(tb + 1) * 512], in_=pproj[pb][:], func=AF.Copy,
                                                       scale=0.125), r=['pproj%d' % pb], w=['qT'])
                for tb in range(9):
                    pb = (tb + 1) % 2
                    src_ = hTo[:, :, tb * 512:(tb + 1) * 512] if tb < 8 else hTh[:, :, :]
                    k.grp('pe', [(lambda e, kc=kc: e.matmul(out=pproj[pb][:], lhsT=wk[b][:, kc, :], rhs=src_[:, kc, :],
                                                              start=(kc == 0), stop=(kc == 7))) for kc in range(8)],
                          r=['wk%d' % b], w=['pproj%d' % pb])
                    k.op('act', lambda e: e.activation(out=kT[:, tb * 512:(tb + 1) * 512], in_=pproj[pb][:], func=AF.Copy),
                         r=['pproj%d' % pb], w=['kT'])
                for t4 in range(9):
                    pb = t4 % 2
                    src_ = hTo[:, :, t4 * 512:(t4 + 1) * 512] if t4 < 8 else hTh[:, :, :]
                    fns = []
                    for tt in range(4):
                        for kc in range(8):
                            fns.append(lambda e, tt=tt, kc=kc: e.matmul(
                                out=pproj[pb][:, tt * P:(tt + 1) * P], lhsT=src_[:, kc, tt * P:(tt + 1) * P], rhs=wv[b][:, kc, :],
                                start=(kc == 0), stop=(kc == 7)))
                    k.grp('pe', fns, r=['wv%d' % b], w=['pproj%d' % pb])
                    k.op('dve', lambda e: e.tensor_copy(
                        out=vS[:, t4 * 4:(t4 + 1) * 4, :, 0:64],
                        in_=pproj[pb][:].rearrange("p (t h d) -> p t h d", t=4, h=2)),
                         r=['pproj%d' % pb], w=['vS'])
                pairs = [(p, h) for p in range(NT) for h in range(2)]

                def emit_scores(kk):
                    p, h = pairs[kk]
                    sbi = kk % 2
                    hb = 64 * h
                    dl = NA_CLASS_DELTAS[na_class(p)]
                    fns = []
                    for bi, dlt in enumerate(dl):
                        kt = mt(p + dlt + 2)
                        dst = psA[sbi][:, bi, :] if bi < 4 else psB[sbi][:, bi - 4, :]
                        fns.append(lambda e, dst=dst, kt=kt: e.matmul(
                            out=dst, lhsT=kT[hb:hb + 64, kt * P:(kt + 1) * P], rhs=qT[hb:hb + 64, p * P:(p + 1) * P],
                            start=True, stop=True))
                    k.grp('pe', fns, r=['qT', 'kT'], w=['psA%d' % sbi, 'psB%d' % sbi])

                emit_scores(0)
                for kk, (p, h) in enumerate(pairs):
                    if kk + 1 < len(pairs):
                        emit_scores(kk + 1)
                    cl = na_class(p)
                    dl = NA_CLASS_DELTAS[cl]
                    nb = len(dl)
                    ob = p % 2
                    sbi = kk % 2
                    k.op('dve', lambda e: e.tensor_tensor(out=tS[sbi][:, 0:4, :], in0=psA[sbi][:], in1=comb[b][:, h, cl, 0:4, :],
                                                          op=ALU.add), r=['psA%d' % sbi, 'comb%d' % b], w=['tS%d' % sbi])
                    k.op('dve', lambda e: e.tensor_tensor(out=tS[sbi][:, 4:nb, :], in0=psB[sbi][:, 0:nb - 4, :],
                                                          in1=comb[b][:, h, cl, 4:nb, :], op=ALU.add),
                         r=['psB%d' % sbi, 'comb%d' % b], w=['tS%d' % sbi])
                    k.op('act', lambda e: e.activation(out=pT_[sbi][:, 0:nb, :], in_=tS[sbi][:, 0:nb, :], func=AF.Exp),
                         r=['tS%d' % sbi], w=['pT%d' % sbi])
                    fns = []
                    for bi, dlt in enumerate(dl):
                        kt = mt(p + dlt + 2)
                        fns.append(lambda e, bi=bi, kt=kt: e.matmul(
                            out=po[:, ob, h, :], lhsT=pT_[sbi][:, bi, :], rhs=vS[:, kt, h, :],
                            start=(bi == 0), stop=(bi == nb - 1)))
                    k.grp('pe', fns, r=['pT%d' % sbi, 'vS'], w=['po%d' % ob])
                    if h == 1:
                        k.op('dve', lambda e: e.reciprocal(out=rinv[ob][:], in_=po[:, ob, :, 64]), r=['po%d' % ob], w=['rinv%d' % ob])
                        for h2 in range(2):
                            k.op('dve', lambda e, h2=h2: e.tensor_scalar(out=yt[ob][:, h2 * 64:(h2 + 1) * 64], in0=po[:, ob, h2, 0:64],
                                                                         scalar1=rinv[ob][:, h2:h2 + 1], scalar2=None, op0=ALU.mult),
                                 r=['po%d' % ob, 'rinv%d' % ob], w=['yt%d' % ob])
                        k.op('pe', lambda e: e.transpose(out=pyt[:, ob, :], in_=yt[ob][:], identity=ident[:]),
                             r=['yt%d' % ob], w=['pyt%d' % ob])
                        k.op('act', lambda e: e.activation(out=yT[b][:, p * P:(p + 1) * P], in_=pyt[:, ob, :], func=AF.Copy),
                             r=['pyt%d' % ob], w=['yT%d' % b])
                k.dma('sp', ynaT_d[hp], yT[b][:], r=['yT%d' % b])
            k.barrier()
        if stop == 'C':
            stH.close(); stB.close()
            return nc

        stH.close()
        with ExitStack() as st:
            lgt = sb("lgt", [P, 8], F32, st)
            dcs = sb("dcs", [P, 4, P], F32, st)
            xie = sb("xie", [P, 2, 512], F32, st)
            zex = sb("zex", [P, 2], F32, st)
            wex = sb("wex", [P, 2, NT], F32, st)
            ccds = sb("ccds", [P, 16], F32, st)
            ccvs = sb("ccvs", [P, 16], F32, st)
            gnr = sb("gnr", [P, 512], F32, st)
            DT = sb("DT", [P, P], F32, st)
            tmpD = sb("tmpD", [P, P], F32, st)
            xiT = sb("xiT", [P, 2, 512], F32, st)
            zT = sb("zT", [P, 2], F32, st)
            wT = sb("wT", [P, 2, NT], F32, st)
            gC = sb("gC", [P, 2], F32, st)
            wcc = sb("wcc", [P, 16], F32, st)
            mark = sb("mark", [P, 1], F32, st)
            wA = sb("wAD", [P, 8, 512], BF16, st)
            wB = sb("wBD", [P, 8, 256], BF16, st)
            kTD = sb("kTD", [P, 2, TOK], BF16, st)
            vD = sb("vD", [P, NT, 512], BF16, st)
            cs_ = sb("csD", [P, 2, 512], F32, st)
            scr = sb("scrD", [P, 4096], F32, st)
            rt = [scr[:, i * 512:(i + 1) * 512] for i in range(4)]
            sloc = scr[:, 0:2048]
            gbuf = scr[:, 2048:4096]
            ktf = [sb("ktfD%d" % i, [P, 256], BF16, st) for i in range(2)]
            ktb = [sb("ktbD%d" % i, [P, 256], BF16, st) for i in range(2)]
            Sst = sb("Sst", [P, 2, 1024], F32, st)
            Sbf = [sb("SbfD%d" % i, [P, 1024], BF16, st) for i in range(2)]
            Sbb = [sb("SbbD%d" % i, [P, 1024], BF16, st) for i in range(2)]
            qTD = sb("qTD", [P, 3, 2, 512], BF16, st)
            Am = [sb("AmD%d" % i, [P, P], BF16, st) for i in range(2)]
            stt = [sb("sttD%d" % i, [P, 6], F32, st) for i in range(2)]
            mv = [sb("mvD%d" % i, [P, 2], F32, st) for i in range(2)]
            rsd = [sb("rsdD%d" % i, [P, 1], F32, st) for i in range(2)]
            ynD = [sb("ynD%d" % i, [P, 512], F32, st) for i in range(2)]
            sgD = [sb("sgD%d" % i, [P, 512], F32, st) for i in range(2)]
            ypD = [sb("ypD%d" % i, [P, 512], BF16, st) for i in range(2)]
            yTs = sb("yTsD", [P, 4, 512], BF16, st)
            pq = [ps("pqD%d" % i, [P, 512], F32, st) for i in range(2)]
            pS = ps("pSD", [P, 4, 512], F32, st)
            pTr = ps("pTrD", [P, 1024], BF16, st)
            pmix = ps("pmixD", [P, 512], F32, st)
            k.dma('sp', lgt[:], dlog, w=['lgt'])
            k.dma('sp', dcs[:], dconst, w=['dcs'])
            k.dma('sp', xie[:], xiexp, w=['xie'])
            k.dma('sp', zex[:], zexp, w=['zex'])
            k.dma('sp', wex[:], wexp, w=['wex'])
            k.dma('sp', ccds[:], ccd, w=['ccds'])
            k.dma('sp', ccvs[:], ccv, w=['ccvs'])
            gF = sb("gFD", [P, 512], F32, st)
            gB = sb("gBD", [P, 512], F32, st)
            c128 = sb("c128D", [P, 1], F32, st)
            k.op('dve', lambda e: e.memset(c128[:], 128.0), w=['c128'])
            k.op('act', lambda e: e.activation(out=lgt[:], in_=lgt[:], func=AF.Sigmoid), r=['lgt'], w=['lgt'])
            k.barrier()
            if os.environ.get('STOPD') == '0':
                st.close(); stB.close(); return nc

            for h in range(4):
                lf = lgt[:, h:h + 1]
                lb = lgt[:, 4 + h:5 + h]
                k.dma('sp', gnr[:], gn_rep[:, h * 512:(h + 1) * 512], w=['gnr'])
                qo, ko, vo, go = 3 * D + h * 256, 4 * D + h * 256, 5 * D + h * 512, 5 * D + 2048 + h * 512
                k.dma('pool', wB[:], w_in[:, ko:ko + 256].rearrange("(kc p) c -> p kc c", p=P), w=['wB'])
                k.dma('pool', wA[:], w_in[:, vo:vo + 512].rearrange("(kc p) c -> p kc c", p=P), w=['wA'])
                def rope_block(tb, dst_fn, dkey, pp=None, pk=None):
                    if pp is None:
                        pp, pk = [pq[0][:], pq[1][:]], ['pq0', 'pq1']
                    k.dma('sp', cs_[:, 0, :], cos_t[:, tb * 512:(tb + 1) * 512], w=['cs'])
                    k.dma('sp', cs_[:, 1, :], sin_t[:, tb * 512:(tb + 1) * 512], w=['cs'])
                    for half in range(2):
                        k.grp('pe', [(lambda e, kc=kc, half=half: e.matmul(
                            out=pp[half], lhsT=wB[:, kc, half * P:(half + 1) * P],
                            rhs=hTo[:, kc, tb * 512:(tb + 1) * 512], start=(kc == 0), stop=(kc == 7)))
                            for kc in range(8)], r=['wB'], w=[pk[half]])
                    c_, s_ = cs_[:, 0, :], cs_[:, 1, :]
                    k.op('dve', lambda e: e.tensor_tensor(out=rt[0], in0=pp[0], in1=c_, op=ALU.mult), r=[pk[0], 'cs'], w=['rt0'])
                    k.op('dve', lambda e: e.tensor_tensor(out=rt[1], in0=pp[1], in1=s_, op=ALU.mult), r=[pk[1], 'cs'], w=['rt1'])
                    k.op('dve', lambda e: e.tensor_tensor(out=rt[2], in0=pp[0], in1=s_, op=ALU.mult), r=[pk[0], 'cs'], w=['rt2'])
                    k.op('dve', lambda e: e.tensor_tensor(out=rt[3], in0=pp[1], in1=c_, op=ALU.mult), r=[pk[1], 'cs'], w=['rt3'])
                    k.op('pool', lambda e: e.tensor_tensor(out=dst_fn(0), in0=rt[0], in1=rt[1], op=ALU.subtract),
                         r=['rt0', 'rt1'], w=[dkey])
                    k.op('pool', lambda e: e.tensor_tensor(out=dst_fn(1), in0=rt[2], in1=rt[3], op=ALU.add),
                         r=['rt2', 'rt3'], w=[dkey])

                for tb in range(8):
                    if tb % 2 == 0:
                        rope_block(tb, lambda half, tb=tb: kTD[:, half, tb * 512:(tb + 1) * 512], 'kTD')
                    else:
                        rope_block(tb, lambda half, tb=tb: kTD[:, half, tb * 512:(tb + 1) * 512], 'kTD',
                                   [pS[:, 0, :], pS[:, 1, :]], ['pSa0', 'pSa1'])
                for t in range(NT):
                    pb = t % 2
                    k.grp('pe', [(lambda e, kc=kc: e.matmul(out=pS[:, 2 + pb, :], lhsT=hTo[:, kc, t * P:(t + 1) * P],
                                                              rhs=wA[:, kc, :], start=(kc == 0), stop=(kc == 7)))
                                 for kc in range(8)], r=['wA'], w=['pSv%d' % pb])
                    k.op('act', lambda e: e.activation(out=vD[:, t, :], in_=pS[:, 2 + pb, :], func=AF.Copy), r=['pSv%d' % pb], w=['vD'])
                k.op('dve', lambda e: e.tensor_scalar(out=gF[:], in0=xie[:, 0, :], scalar1=0.0, scalar2=lf, op0=ALU.mult, op1=ALU.add),
                     w=['gF'])
                k.op('dve', lambda e: e.tensor_scalar(out=gB[:], in0=xie[:, 0, :], scalar1=0.0, scalar2=lb, op0=ALU.mult, op1=ALU.add),
                     w=['gB'])

                def ppow(out, base, expo, key):
                    k.op('pool', lambda e: e.tensor_tensor(out=out, in0=base, in1=expo, op=ALU.pow), r=['gF', 'gB'], w=[key])

                ppow(DT[:], gF[:, 0:P], dcs[:, 0, :], 'DT')
                k.op('dve', lambda e: e.scalar_tensor_tensor(out=DT[:], in0=DT[:], scalar=1.0 / 16, in1=dcs[:, 1, :],
                                                             op0=ALU.mult, op1=ALU.mult), r=['DT'], w=['DT'])
                ppow(tmpD[:], gB[:, 0:P], dcs[:, 2, :], 'tmpD')
                k.op('dve', lambda e: e.scalar_tensor_tensor(out=tmpD[:], in0=tmpD[:], scalar=1.0 / 16, in1=dcs[:, 3, :],
                                                             op0=ALU.mult, op1=ALU.mult), r=['tmpD'], w=['tmpD'])
                k.op('dve', lambda e: e.tensor_tensor(out=DT[:], in0=DT[:], in1=tmpD[:], op=ALU.add), r=['DT', 'tmpD'], w=['DT'])
                for d_, gg_ in ((0, gF), (1, gB)):
                    ppow(xiT[:, d_, :], gg_[:], xie[:, d_, :], 'xiT')
                    ppow(zT[:, d_:d_ + 1], gg_[:, 0:1], zex[:, d_:d_ + 1], 'zT')
                    ppow(wT[:, d_, :], gg_[:, 0:NT], wex[:, d_, :], 'wT')
                    ppow(gC[:, d_:d_ + 1], gg_[:, 0:1], c128[:], 'gC')
                    ppow(wcc[:, d_ * 8:(d_ + 1) * 8], gg_[:, 0:8], ccds[:, d_ * 8:(d_ + 1) * 8], 'wcc')
                k.op('dve', lambda e: e.tensor_scalar(out=xiT[:], in0=xiT[:], scalar1=1.0 / 16, scalar2=None, op0=ALU.mult),
                     r=['xiT'], w=['xiT'])
                k.op('dve', lambda e: e.tensor_scalar(out=wT[:], in0=wT[:], scalar1=1e-30, scalar2=None, op0=ALU.max), r=['wT'], w=['wT'])
                k.op('dve', lambda e: e.tensor_scalar(out=wcc[:], in0=wcc[:], scalar1=1e-30, scalar2=None, op0=ALU.max), r=['wcc'], w=['wcc'])
                k.op('dve', lambda e: e.tensor_tensor(out=wcc[:], in0=wcc[:], in1=ccvs[:], op=ALU.mult), r=['wcc'], w=['wcc'])

                if os.environ.get('STOPD') == '1':
                    k.barrier(); st.close(); stB.close(); return nc

                k.barrier()
                if os.environ.get('STOPD') == '2':
                    st.close(); stB.close(); return nc
                if mode != 'pre':
                    k.dma('pool', wB[:], w_in[:, qo:qo + 256].rearrange("(kc p) c -> p kc c", p=P), w=['wB'])
                    k.dma('pool', wA[:], w_in[:, go:go + 512].rearrange("(kc p) c -> p kc c", p=P), w=['wA'])

                def ktrans(c, scf, scb, bsel):
                    k.grp('pe', [(lambda e, dh=dh: e.transpose(out=pTr[:, dh * P:(dh + 1) * P], in_=kTD[:, dh, c * P:(c + 1) * P],
                                                               identity=ident[:])) for dh in range(2)], r=['kTD'], w=['pTrk'])
                    if os.environ.get('KTNOEV'):
                        return
                    if scf is not None:
                        if os.environ.get('KTPLAIN'):
                            k.op('act', lambda e: e.activation(out=ktf[bsel][:], in_=pTr[:, 0:256], func=AF.Copy),
                                 r=['pTrk'], w=['ktf%d' % bsel])
                        else:
                            if scb is not None:
                                k.op('dve', lambda e: e.tensor_scalar(out=ktf[bsel][:], in0=pTr[:, 0:256], scalar1=scf, scalar2=None,
                                                                      op0=ALU.mult), r=['pTrk'], w=['ktf%d' % bsel])
                            else:
                                k.op('act', lambda e: e.activation(out=ktf[bsel][:], in_=pTr[:, 0:256], func=AF.Copy, scale=scf),
                                     r=['pTrk'], w=['ktf%d' % bsel])
                    if scb is not None:
                        if os.environ.get('KTPLAIN'):
                            k.op('dve', lambda e: e.tensor_copy(out=ktb[bsel][:], in_=pTr[:, 0:256]), r=['pTrk'], w=['ktb%d' % bsel])
                        else:
                            k.op('dve', lambda e: e.tensor_scalar(out=ktb[bsel][:], in0=pTr[:, 0:256], scalar1=scb, scalar2=None,
                                                                  op0=ALU.mult), r=['pTrk'], w=['ktb%d' % bsel])

                if mode != 'main':
                    ktrans(0, wT[:, 0, 0:1], wT[:, 1, 0:1], 0)
                    for c in range(NT):
                        bs = c % 2
                        if c + 1 < NT:
                            ktrans(c + 1, wT[:, 0, c + 1:c + 2], wT[:, 1, c + 1:c + 2], 1 - bs)
                        fns = []
                        for dh in range(2):
                            fns.append(lambda e, dh=dh: e.matmul(out=pS[:, dh, :], lhsT=ktf[bs][:, dh * P:(dh + 1) * P], rhs=vD[:, c, :],
                                                                 start=(c == 0), stop=(c == NT - 1)))
                            fns.append(lambda e, dh=dh: e.matmul(out=pS[:, 2 + dh, :], lhsT=ktb[bs][:, dh * P:(dh + 1) * P], rhs=vD[:, c, :],
                                                                 start=(c == 0), stop=(c == NT - 1)))
                        if os.environ.get('D2SKIP') != 'mm':
                            k.grp('pe', fns, r=['ktf%d' % bs, 'ktb%d' % bs], w=['pS'])
                    for q4 in range(4 if os.environ.get('D2SKIP') != 'mm' else 0):
                        if q4 % 2 == 0:
                            k.op('act', lambda e, q4=q4: e.activation(out=sloc[:, q4 * 512:(q4 + 1) * 512], in_=pS[:, q4, :], func=AF.Copy),
                                 r=['pS'], w=['sloc'])
                        else:
                            k.op('dve', lambda e, q4=q4: e.tensor_copy(out=sloc[:, q4 * 512:(q4 + 1) * 512], in_=pS[:, q4, :]),
                                 r=['pS'], w=['sloc'])
                if mode == 'pre':
                    k.dma('sp', sloc_out[h], sloc, r=['sloc'])
                    k.barrier()
                    if os.environ.get('STOPD') == '3':
                        st.close(); stB.close(); return nc
                    continue
                if mode == 'fused':
                    k.dma('pool', sloc_h[h].ap(), sloc, r=['sloc'], w=['sloc_d'])
                    k._wait('pool', k._deps(['sloc_d'], []))
                    nc.gpsimd.collective_compute("AllGather", ALU.bypass, replica_groups=[[0, 1, 2, 3], [4, 5, 6, 7]],
                                                 ins=[sloc_h[h].ap().opt()], outs=[gath_h[h].ap().opt()]).then_inc(cc_sem, 1)
                    nc.gpsimd.wait_ge(cc_sem, h + 1)
                    k.op('pool', lambda e: e.memset(mark[:], 0.0), w=['gath_d'])
                    gv = gath_h[h].ap()
                else:
                    gv = gath_in[h]
                for i in range(4):
                    k.dma('sp', gbuf, gv[i * P:(i + 1) * P, :], r=['gath_d'], w=['gbuf'])
                    for d_ in range(2):
                        src = gbuf[:, d_ * 1024:(d_ + 1) * 1024]
                        wc = wcc[:, d_ * 8 + i:d_ * 8 + i + 1]
                        if i == 0:
                            k.op('dve', lambda e, d_=d_, src=src, wc=wc: e.tensor_scalar(out=Sst[:, d_, :], in0=src, scalar1=wc,
                                                                                       scalar2=None, op0=ALU.mult),
                                 r=['gbuf', 'wcc'], w=['Sst%d' % d_])
                        else:
                            k.op('dve', lambda e, d_=d_, src=src, wc=wc: e.scalar_tensor_tensor(
                                out=Sst[:, d_, :], in0=src, scalar=wc, in1=Sst[:, d_, :], op0=ALU.mult, op1=ALU.add),
                                 r=['gbuf', 'wcc'], w=['Sst%d' % d_])
                k.barrier()
                ktrans(NT - 1, None, zT[:, 1:2], (NT - 1) % 2)
                for c in range(NT - 1, -1, -1):
                    bs = c % 2
                    k.op('act', lambda e: e.activation(out=Sbb[bs][:], in_=Sst[:, 1, :], func=AF.Copy), r=['Sst1'], w=['Sbb%d' % bs])
                    k.dma('sp', sb_d[c], Sbb[bs][:], r=['Sbb%d' % bs], w=[('sb_d', c)])
                    if c > 0:
                        ktrans(c - 1, None, zT[:, 1:2], 1 - bs)
                    pb_ = 2 * bs
                    k.grp('pe', [(lambda e, dh=dh: e.matmul(out=pS[:, pb_ + dh, :], lhsT=ktb[bs][:, dh * P:(dh + 1) * P], rhs=vD[:, c, :],
                                                              start=True, stop=True)) for dh in range(2)],
                          r=['ktb%d' % bs], w=['pSb%d' % bs])
                    for dh in range(2):
                        k.op('dve', lambda e, dh=dh: e.scalar_tensor_tensor(
                            out=Sst[:, 1, dh * 512:(dh + 1) * 512], in0=Sst[:, 1, dh * 512:(dh + 1) * 512], scalar=gC[:, 1:2],
                            in1=pS[:, pb_ + dh, :], op0=ALU.mult, op1=ALU.add), r=['pSb%d' % bs, 'Sst1'], w=['Sst1'])
                k.op('act', lambda e: e.activation(out=Sbf[0][:], in_=Sst[:, 0, :], func=AF.Copy), r=['Sst0'], w=['Sbf0'])
                pend_y = []

                def emit_ytr():
                    while pend_y:
                        cc_, bs_ = pend_y.pop(0)
                        k.grp('pe', [(lambda e, fc=fc: e.transpose(out=pTr[:, 512 + fc * P:512 + (fc + 1) * P],
                                                                   in_=ypD[bs_][:, fc * P:(fc + 1) * P], identity=ident[:]))
                                     for fc in range(4)], r=['ypD%d' % bs_], w=['pTry'])
                        k.op('act', lambda e: e.activation(out=yTs[:, :, cc_ * P:(cc_ + 1) * P],
                                                           in_=pTr[:, 512:1024].rearrange("p (f t) -> p f t", f=4), func=AF.Copy),
                             r=['pTry'], w=['yTs'])

                for tb in range(8):
                    rope_block(tb, lambda half: qTD[:, 0, half, :], 'qTD')
                    for d_ in range(2):
                        for half in range(2):
                            k.op('pool', lambda e, d_=d_, half=half: e.tensor_tensor(
                                out=qTD[:, 1 + d_, half, :], in0=qTD[:, 0, half, :], in1=xiT[:, d_, :], op=ALU.mult),
                                 r=['qTD', 'xiT'], w=['qTD'])
                    for cc in range(4):
                        c = tb * 4 + cc
                        bs = c % 2
                        k.dma('sp', Sbb[bs][:], sb_d[c], r=[('sb_d', c)], w=['Sbb%d' % bs])
                        k.grp('pe', [(lambda e, dh=dh: e.matmul(out=pmix[:, 0:P], lhsT=kTD[:, dh, c * P:(c + 1) * P],
                                                                  rhs=qTD[:, 0, dh, cc * P:(cc + 1) * P],
                                                                  start=(dh == 0), stop=(dh == 1))) for dh in range(2)],
                              r=['qTD'], w=['pmix'])
                        k.op('dve', lambda e: e.tensor_tensor(out=Am[bs][:], in0=pmix[:, 0:P], in1=DT[:], op=ALU.mult),
                             r=['pmix', 'DT'], w=['Am%d' % bs])
                        k.grp('pe', [(lambda e, kc=kc: e.matmul(out=pq[1][:], lhsT=hTo[:, kc, c * P:(c + 1) * P],
                                                                  rhs=wA[:, kc, :], start=(kc == 0), stop=(kc == 7)))
                                     for kc in range(8)], r=['wA'], w=['pq1'])
                        fns = [lambda e: e.matmul(out=pq[0][:], lhsT=Am[bs][:], rhs=vD[:, c, :], start=True, stop=False)]
                        for dh in range(2):
                            fns.append(lambda e, dh=dh: e.matmul(out=pq[0][:], lhsT=qTD[:, 1, dh, cc * P:(cc + 1) * P],
                                                                 rhs=Sbf[bs][:, dh * 512:(dh + 1) * 512], start=False, stop=False))
                        for dh in range(2):
                            fns.append(lambda e, dh=dh: e.matmul(out=pq[0][:], lhsT=qTD[:, 2, dh, cc * P:(cc + 1) * P],
                                                                 rhs=Sbb[bs][:, dh * 512:(dh + 1) * 512], start=False, stop=(dh == 1)))
                        k.grp('pe', fns, r=['Am%d' % bs, 'qTD', 'Sbf%d' % bs, 'Sbb%d' % bs], w=['pq0'])
                        ktrans(c, zT[:, 0:1], None, bs)
                        pb_ = 2 * bs
                        k.grp('pe', [(lambda e, dh=dh: e.matmul(out=pS[:, pb_ + dh, :], lhsT=ktf[bs][:, dh * P:(dh + 1) * P], rhs=vD[:, c, :],
                                                                  start=True, stop=True)) for dh in range(2)],
                              r=['ktf%d' % bs], w=['pSf%d' % bs])
                        emit_ytr()
                        for dh in range(2):
                            k.op('dve', lambda e, dh=dh: e.scalar_tensor_tensor(
                                out=Sst[:, 0, dh * 512:(dh + 1) * 512], in0=Sst[:, 0, dh * 512:(dh + 1) * 512], scalar=gC[:, 0:1],
                                in1=pS[:, pb_ + dh, :], op0=ALU.mult, op1=ALU.add), r=['pSf%d' % bs, 'Sst0'], w=['Sst0'])
                        k.op('act', lambda e: e.activation(out=Sbf[1 - bs][:], in_=Sst[:, 0, :], func=AF.Copy),
                             r=['Sst0'], w=['Sbf%d' % (1 - bs)])
                        k.op('dve', lambda e: e.bn_stats(out=stt[bs][:], in_=pq[0][:]), r=['pq0'], w=['stt%d' % bs])
                        k.op('dve', lambda e: e.bn_aggr(out=mv[bs][:], in_=stt[bs][:]), r=['stt%d' % bs], w=['mv%d' % bs])
                        k.op('dve', lambda e: e.tensor_scalar(out=ynD[bs][:], in0=pq[0][:], scalar1=mv[bs][:, 0:1],
                                                              scalar2=None, op0=ALU.subtract),
                             r=['pq0', 'mv%d' % bs], w=['ynD%d' % bs])
                        rstd_from_ss(mv[bs][:, 1:2], rsd[bs][:], 'mv%d' % bs, 'rsd%d' % bs, 1.0)
                        k.op('act', lambda e: e.activation(out=sgD[bs][:], in_=pq[1][:], func=AF.Silu), r=['pq1'], w=['sgD%d' % bs])
                        k.op('pool', lambda e: e.tensor_tensor(out=ynD[bs][:], in0=ynD[bs][:], in1=gnr[:], op=ALU.mult),
                             r=['ynD%d' % bs, 'gnr'], w=['ynD%d' % bs])
                        k.op('dve', lambda e: e.scalar_tensor_tensor(out=ypD[bs][:], in0=ynD[bs][:], scalar=rsd[bs][:], in1=sgD[bs][:],
                                                                     op0=ALU.mult, op1=ALU.mult),
                             r=['ynD%d' % bs, 'sgD%d' % bs, 'rsd%d' % bs], w=['ypD%d' % bs])
                        pend_y.append((cc, bs))
                    emit_ytr()
                    k.dma('sp', yretT_d[h * 4:(h + 1) * 4, :, tb * 512:(tb + 1) * 512].rearrange("f p t -> p f t"),
                          yTs[:], r=['yTs'])
                k.barrier()
        stB.close()

        if stop == 'D' or mode == 'pre':
            stB.close()
            return nc
        with ExitStack() as st:
            Wun = sb("Wun", [P, 8, D], BF16, st)
            Wur = sb("Wur", [P, 16, D], BF16, st)
            Wgn = sb("Wgn", [P, 8, D], BF16, st)
            Wgr = sb("Wgr", [P, 8, D], BF16, st)
            Wo = sb("Wo", [P, 8, D], BF16, st)
            gg1 = sb("gg1E", [P, D], F32, st)
            ynT = sb("ynTE", [P, 8, 512], BF16, st)
            yrT = sb("yrTE", [P, 16, 512], BF16, st)
            hTt = sb("hTtE", [P, 8, 512], BF16, st)
            sgn = sb("sgnE", [P, 512], F32, st)
            sgr = sb("sgrE", [P, 512], F32, st)
            m1 = sb("m1E", [P, 512], F32, st)
            m2 = sb("m2E", [P, 512], F32, st)
            mT = sb("mTE", [P, 8, 512], BF16, st)
            xt = [sb("xtE%d" % i, [P, D], F32, st) for i in range(2)]
            tt_ = [sb("ttE%d" % i, [P, D], F32, st) for i in range(2)]
            oc_ = [sb("ocE%d" % i, [P, D], F32, st) for i in range(2)]
            x1 = [sb("x1E%d" % i, [P, D], F32, st) for i in range(2)]
            xn2 = [sb("xn2E%d" % i, [P, D], BF16, st) for i in range(2)]
            ssE = [sb("ssE%d" % i, [P, 1], F32, st) for i in range(2)]
            ssh = [sb("sshE%d" % i, [P, 2], F32, st) for i in range(2)]
            rsE = [sb("rsE%d" % i, [P, 1], F32, st) for i in range(2)]
            ss2 = [sb("ss2E%d" % i, [P, 1], F32, st) for i in range(2)]
            rs2 = [sb("rs2E%d" % i, [P, 1], F32, st) for i in range(2)]
            h2s = sb("h2sE", [P, 8, 512], BF16, st)
            pU = [ps("pUE%d" % i, [P, 512], F32, st) for i in range(4)]
            pO = ps("pOE", [P, 2, 512], F32, st)
            pst2 = ps("pst2E", [P, 8, P], BF16, st)
            k.dma('sp', gg1[:], gg_d[0], w=['gg1'])
            for (dst, src, nm) in ((Wun, w_up_na, 'Wun'), (Wur, w_up_ret, 'Wur'),
                                   (Wgn, w_in[:, 9 * D:10 * D], 'Wgn'), (Wgr, w_in[:, 10 * D:11 * D], 'Wgr'),
                                   (Wo, w_out, 'Wo')):
                k.dma('pool', dst[:], src.rearrange("(kc p) c -> p kc c", p=P), w=[nm])

            pendT = []
            for tb in range(8):
                k.dma('sp', ynT[:], ynaT_d[:, :, tb * 512:(tb + 1) * 512].rearrange("f p t -> p f t"), w=['ynT'])
                k.dma('sp', yrT[:], yretT_d[:, :, tb * 512:(tb + 1) * 512].rearrange("f p t -> p f t"), w=['yrT'])
                k.dma('sp', hTt[:], hT_d[:, :, tb * 512:(tb + 1) * 512].rearrange("f p t -> p f t"), w=['hTt'])
                for ob in range(8):
                    osl = slice(ob * P, (ob + 1) * P)
                    k.grp('pe', [(lambda e, fc=fc: e.matmul(out=pU[0][:], lhsT=Wun[:, fc, osl], rhs=ynT[:, fc, :],
                                                              start=(fc == 0), stop=(fc == 7))) for fc in range(8)],
                          r=['Wun', 'ynT'], w=['pU0'])
                    k.grp('pe', [(lambda e, fc=fc: e.matmul(out=pU[1][:], lhsT=Wur[:, fc, osl], rhs=yrT[:, fc, :],
                                                              start=(fc == 0), stop=(fc == 15))) for fc in range(16)],
                          r=['Wur', 'yrT'], w=['pU1'])
                    k.grp('pe', [(lambda e, fc=fc: e.matmul(out=pU[2][:], lhsT=Wgn[:, fc, osl], rhs=hTt[:, fc, :],
                                                              start=(fc == 0), stop=(fc == 7))) for fc in range(8)],
                          r=['Wgn', 'hTt'], w=['pU2'])
                    k.grp('pe', [(lambda e, fc=fc: e.matmul(out=pU[3][:], lhsT=Wgr[:, fc, osl], rhs=hTt[:, fc, :],
                                                              start=(fc == 0), stop=(fc == 7))) for fc in range(8)],
                          r=['Wgr', 'hTt'], w=['pU3'])
                    k.op('act', lambda e: e.activation(out=sgn[:], in_=pU[2][:], func=AF.Sigmoid), r=['pU2'], w=['sgn'])
                    k.op('act', lambda e: e.activation(out=sgr[:], in_=pU[3][:], func=AF.Sigmoid), r=['pU3'], w=['sgr'])
                    k.op('dve', lambda e: e.tensor_tensor(out=m1[:], in0=pU[0][:], in1=sgn[:], op=ALU.mult),
                         r=['pU0', 'sgn'], w=['m1'])
                    k.op('dve', lambda e: e.tensor_tensor(out=m2[:], in0=pU[1][:], in1=sgr[:], op=ALU.mult),
                         r=['pU1', 'sgr'], w=['m2'])
                    k.op('pool', lambda e: e.tensor_tensor(out=mT[:, ob, :], in0=m1[:], in1=m2[:], op=ALU.add),
                         r=['m1', 'm2'], w=['mT'])
                for ts in range(4):
                    t = tb * 4 + ts
                    tb2 = t % 2
                    k.dma('sp', xt[tb2][:], x_ext[HALO + t * P:HALO + (t + 1) * P, :], w=['xt%d' % tb2])
                    for half in range(2):
                        k.grp('pe', [(lambda e, fc=fc, half=half: e.matmul(out=pO[:, half, :], lhsT=mT[:, fc, ts * P:(ts + 1) * P],
                                                                            rhs=Wo[:, fc, half * 512:(half + 1) * 512],
                                                                            start=(fc == 0), stop=(fc == 7))) for fc in range(8)],
                              r=['mT', 'Wo'], w=['pO'])
                    while pendT:
                        pendT.pop(0)()
                    for half in range(2):
                        k.op('act', lambda e, half=half: e.activation(out=tt_[tb2][:, half * 512:(half + 1) * 512], in_=pO[:, half, :],
                                                                      func=AF.Square, accum_out=ssh[tb2][:, half:half + 1]),
                             r=['pO'], w=['tt%d' % tb2, 'ssh%d' % tb2])
                        k.op('act', lambda e, half=half: e.activation(out=oc_[tb2][:, half * 512:(half + 1) * 512], in_=pO[:, half, :],
                                                                      func=AF.Copy), r=['pO'], w=['oc%d' % tb2])
                    k.op('dve', lambda e: e.tensor_tensor(out=ssE[tb2][:], in0=ssh[tb2][:, 0:1], in1=ssh[tb2][:, 1:2], op=ALU.add),
                         r=['ssh%d' % tb2], w=['ssE%d' % tb2])
                    rstd_from_ss(ssE[tb2][:], rsE[tb2][:], 'ssE%d' % tb2, 'rsE%d' % tb2, 1.0 / D)
                    k.op('dve', lambda e: e.scalar_tensor_tensor(
                        out=tt_[tb2][:], in0=oc_[tb2][:], scalar=rsE[tb2][:], in1=gg1[:], op0=ALU.mult, op1=ALU.mult),
                         r=['oc%d' % tb2, 'rsE%d' % tb2, 'gg1'], w=['tt%d' % tb2])
                    k.op('pool', lambda e: e.tensor_tensor(out=x1[tb2][:], in0=tt_[tb2][:], in1=xt[tb2][:], op=ALU.add),
                         r=['tt%d' % tb2, 'xt%d' % tb2], w=['x1%d' % tb2])
                    k.dma('sp', x1_d[t * P:(t + 1) * P, :], x1[tb2][:], r=['x1%d' % tb2])
                    norm_to_T(x1[tb2][:], 'x1%d' % tb2, ss2[tb2], rs2[tb2], xn2[tb2], pst2, 'pst2', gs2, sh2,
                              lambda kc, ts=ts: h2s[:, kc, ts * P:(ts + 1) * P], 'h2s', 'E%d' % tb2, defer=pendT)
                while pendT:
                    pendT.pop(0)()
                k.dma('sp', h2T_d[:, :, tb * 512:(tb + 1) * 512].rearrange("f p t -> p f t"), h2s[:], r=['h2s'])
            k.barrier()
        stB.close()
        if stop == 'Ea':
            return nc

        with ExitStack() as st:
            Wfi = sb("Wfi", [P, 8, 2 * FFH], BF16, st)
            Wfo = sb("Wfo", [P, 22, D], BF16, st)
            gg2 = sb("gg2F", [P, D], F32, st)
            h2t = sb("h2tF", [P, 8, 512], BF16, st)
            sa = [sb("saF%d" % i, [P, 512], F32, st) for i in range(2)]
            hid = sb("hidF", [P, 22, 512], BF16, st)
            x1t = sb("x1tF", [P, D], F32, st)
            tf = sb("tfF", [P, D], F32, st)
            yo = sb("yoF", [P, D], F32, st)
            ssF = [sb("ssF%d" % i, [P, 1], F32, st) for i in range(2)]
            sshF = [sb("sshF%d" % i, [P, 2], F32, st) for i in range(2)]
            rsF = [sb("rsF%d" % i, [P, 1], F32, st) for i in range(2)]
            pA = [ps("pAF%d" % i, [P, 512], F32, st) for i in range(2)]
            pG = [ps("pGF%d" % i, [P, 512], F32, st) for i in range(2)]
            pF = [ps("pFF%d" % i, [P, 2, 512], F32, st) for i in range(2)]
            k.dma('sp', gg2[:], gg_d[1], w=['gg2'])
            for part in range(4):
                k.dma('pool', Wfi[:, :, part * 1408:(part + 1) * 1408],
                      w_ffn_in[:, part * 1408:(part + 1) * 1408].rearrange("(kc p) c -> p kc c", p=P), w=['Wfi'])
            k.dma('pool', Wfo[:], w_ffn_out.rearrange("(fc p) c -> p fc c", p=P), w=['Wfo'])
            for tb in range(8):
                k.dma('sp', h2t[:], h2T_d[:, :, tb * 512:(tb + 1) * 512].rearrange("f p t -> p f t"), w=['h2t'])
                for fb in range(22):
                    f2 = fb % 2
                    k.grp('pe', [(lambda e, kc=kc: e.matmul(out=pA[f2][:], lhsT=Wfi[:, kc, fb * P:(fb + 1) * P], rhs=h2t[:, kc, :],
                                                              start=(kc == 0), stop=(kc == 7))) for kc in range(8)],
                          r=['Wfi', 'h2t'], w=['pA%d' % f2])
                    k.grp('pe', [(lambda e, kc=kc: e.matmul(out=pG[f2][:], lhsT=Wfi[:, kc, FFH + fb * P:FFH + (fb + 1) * P],
                                                              rhs=h2t[:, kc, :], start=(kc == 0), stop=(kc == 7))) for kc in range(8)],
                          r=['Wfi', 'h2t'], w=['pG%d' % f2])
                    k.op('act', lambda e: e.activation(out=sa[f2][:], in_=pA[f2][:], func=AF.Silu), r=['pA%d' % f2], w=['sa%d' % f2])
                    k.op('dve', lambda e: e.tensor_tensor(out=hid[:, fb, :], in0=pG[f2][:], in1=sa[f2][:], op=ALU.mult),
                         r=['pG%d' % f2, 'sa%d' % f2], w=['hid'])
                for ts in range(4):
                    t = tb * 4 + ts
                    t2 = t % 2
                    k.dma('sp', x1t[:], x1_d[t * P:(t + 1) * P, :], w=['x1t'])
                    for half in range(2):
                        k.grp('pe', [(lambda e, fc=fc, half=half: e.matmul(out=pF[t2][:, half, :], lhsT=hid[:, fc, ts * P:(ts + 1) * P],
                                                                            rhs=Wfo[:, fc, half * 512:(half + 1) * 512],
                                                                            start=(fc == 0), stop=(fc == 21))) for fc in range(22)],
                              r=['hid', 'Wfo'], w=['pF%d' % t2])
                    for half in range(2):
                        k.op('act', lambda e, half=half: e.activation(out=tf[:, half * 512:(half + 1) * 512], in_=pF[t2][:, half, :],
                                                                      func=AF.Square, accum_out=sshF[t2][:, half:half + 1]),
                             r=['pF%d' % t2], w=['tf', 'sshF%d' % t2])
                    k.op('dve', lambda e: e.tensor_tensor(out=ssF[t2][:], in0=sshF[t2][:, 0:1], in1=sshF[t2][:, 1:2], op=ALU.add),
                         r=['sshF%d' % t2], w=['ssF%d' % t2])
                    rstd_from_ss(ssF[t2][:], rsF[t2][:], 'ssF%d' % t2, 'rsF%d' % t2, 1.0 / D)
                    for half in range(2):
                        k.op('dve', lambda e, half=half: e.scalar_tensor_tensor(
                            out=tf[:, half * 512:(half + 1) * 512], in0=pF[t2][:, half, :], scalar=rsF[t2][:],
                            in1=gg2[:, half * 512:(half + 1) * 512], op0=ALU.mult, op1=ALU.mult),
                             r=['pF%d' % t2, 'rsF%d' % t2, 'gg2'], w=['tf'])
                    k.op('pool', lambda e: e.tensor_tensor(out=yo[:], in0=tf[:], in1=x1t[:], op=ALU.add),
                         r=['tf', 'x1t'], w=['yo'])
                    k.dma('sp', y_out[t * P:(t + 1) * P, :], yo[:], r=['yo'])
            k.barrier()
    return nc


_NC_CACHE = {}


def _host_tables():
    ki = np.arange(P) // 64
    kj = np.arange(P) % 64
    j = np.arange(P)[:, None].astype(np.float32)
    i = np.arange(P)[None, :].astype(np.float32)
    dconst = np.stack([np.maximum(i - j, 0), (i >= j).astype(np.float32), np.maximum(j - i, 0), (j > i).astype(np.float32)],
                      axis=1).astype(np.float32)
    i512 = (np.arange(512) % P).astype(np.float32)
    xiexp = np.broadcast_to(np.stack([i512 + 1, P - i512], 0)[None], (P, 2, 512)).astype(np.float32).copy()
    jj = np.arange(P).astype(np.float32)
    zexp = np.stack([127 - jj, jj], 1).astype(np.float32)
    c = np.arange(NT).astype(np.float32)[None, :]
    wexp = np.stack([4095 - (128 * c + jj[:, None]), 128 * c + jj[:, None]], 1).astype(np.float32)
    return dconst, xiexp, zexp, wexp


def make_in_maps(x_prompt, x_sample, c_prompt, c_sample, w_mod, b_mod, g_pre_mix, w_in, rpb, ret_decay_logit, ret_gn,
           w_up_na, w_up_ret, w_out, g_post_mix, g_pre_ffn, w_ffn_in, w_ffn_out, g_post_ffn):
    f32 = np.float32
    xs = [np.asarray(x_prompt[0], f32), np.asarray(x_prompt[1], f32), np.asarray(x_sample[0], f32)]
    cs = [np.asarray(c_prompt[0], f32), np.asarray(c_prompt[1], f32), np.asarray(c_sample[0], f32)]
    core_seq = [0, 0, 1, 1, 2, 2, 2, 2]
    core_pos = [0, 1, 0, 1, 0, 1, 2, 3]
    seq_cores = [2, 2, 4]
    rpb0 = np.asarray(rpb[0], f32)

    kidx = np.arange(P)
    ki, kj = kidx // 64, kidx % 64
    qi, qc = kidx // 64, kidx % 64
    bias7 = np.zeros((16, P, 7, P), f32)
    for d in range(-3, 4):
        rr = 2 * d + ki[:, None] - qi[None, :] + 7
        cr = kj[:, None] - qc[None, :] + 15
        ok = (rr >= 0) & (rr < 15) & (cr >= 0) & (cr < 31)
        g = rpb0[:, np.clip(rr, 0, 14), np.clip(cr, 0, 30)]
        bias7[:, :, d + 3, :] = np.where(ok[None], g, 0.0)
    dconst, xiexp, zexp, wexp = _host_tables()
    ident = np.eye(P, dtype=f32).astype(ml_dtypes.bfloat16)
    inv = (1.0 / (np.float32(10000.0) ** (np.arange(128, dtype=f32) / np.float32(128)))).astype(f32)

    in_maps = []
    for core in range(NCORES):
        s, pos, ncs = core_seq[core], core_pos[core], seq_cores[core_seq[core]]
        xseq = xs[s]
        t0 = pos * TOK
        x_ext = np.zeros((EXT, D), f32)
        lo, hi = t0 - HALO, t0 + TOK + HALO
        slo, shi = max(lo, 0), min(hi, xseq.shape[0])
        x_ext[slo - lo:shi - lo] = xseq[slo:shi]
        rows_total = xseq.shape[0] // 64
        r0 = t0 // 64
        comb_in = np.full((16, P, 5, 6, P), NEG, f32)
        for cl, prep in ((0, 0), (1, 1), (2, 2), (3, NT - 2), (4, NT - 1)):
            for bi, dlt in enumerate(NA_CLASS_DELTAS[cl]):
                qrow = r0 + 2 * prep + qi[None, :]
                krow = r0 + 2 * (prep + dlt) + ki[:, None]
                rs = np.clip(qrow - 4, 0, rows_total - 8)
                cst = np.clip(qc[None, :] - 8, 0, 48)
                ok = (krow >= 0) & (krow < rows_total) & (krow >= rs) & (krow < rs + 8) & \
                     (kj[:, None] >= cst) & (kj[:, None] < cst + 16)
                comb_in[:, :, cl, bi, :] = np.where(ok[None], bias7[:, :, dlt + 3, :], NEG)
        posv = (t0 + np.arange(TOK)).astype(f32)
        ang = (posv[None, :] * inv[:, None]).astype(f32)
        ccd = np.zeros(16, f32)
        ccv = np.zeros(16, f32)
        base = 0 if core < 4 else 4
        for sl_ in range(4):
            o = base + sl_
            if core_seq[o] == s:
                if o < core:
                    ccd[sl_] = 4096.0 * (core - 1 - o); ccv[sl_] = 1.0
                if o > core:
                    ccd[8 + sl_] = 4096.0 * (o - core - 1); ccv[8 + sl_] = 1.0
        in_maps.append({
            "x_ext": x_ext,
            "cT": np.ascontiguousarray(cs[s].reshape(8, P).T),
            "w_mod": np.asarray(w_mod[0], f32), "b_mod": np.asarray(b_mod, f32).reshape(1, -1),
            "g_rows": np.concatenate([np.asarray(g_pre_mix[0], f32), np.asarray(g_post_mix[0], f32),
                                      np.asarray(g_pre_ffn[0], f32), np.asarray(g_post_ffn[0], f32)]).reshape(1, -1),
            "w_in": np.asarray(w_in[0], f32), "comb_in": comb_in,
            "dlog": np.ascontiguousarray(np.broadcast_to(np.asarray(ret_decay_logit[0], f32).reshape(1, 8), (P, 8))),
            "gn_rep": np.ascontiguousarray(np.broadcast_to(np.asarray(ret_gn[0], f32).reshape(1, -1), (P, 2048))),
            "w_up_na": np.asarray(w_up_na[0], f32), "w_up_ret": np.asarray(w_up_ret[0], f32),
            "w_out": np.asarray(w_out[0], f32), "w_ffn_in": np.asarray(w_ffn_in[0], f32),
            "w_ffn_out": np.asarray(w_ffn_out[0], f32),
            "cos_t": np.cos(ang).astype(f32), "sin_t": np.sin(ang).astype(f32),
            "ident": ident, "dconst": dconst, "xiexp": xiexp, "zexp": zexp, "wexp": wexp,
            "ccd": np.ascontiguousarray(np.broadcast_to(ccd[None], (P, 16))),
            "ccv": np.ascontiguousarray(np.broadcast_to(ccv[None], (P, 16))),
        })
    return in_maps


def kernel(**inputs):
    in_maps = make_in_maps(**inputs)
    if "fused" not in _NC_CACHE:
        _NC_CACHE["fused"] = build(mode='fused')
    res = run_bass_kernel_spmd(_NC_CACHE["fused"], in_maps, core_ids=list(range(NCORES)))
    outs = [np.asarray(r["y_out"], np.float32) for r in res.results]
    y_prompt = np.stack([np.concatenate(outs[0:2], 0), np.concatenate(outs[2:4], 0)], 0)
    y_sample = np.concatenate(outs[4:8], 0)[None]
    return (y_prompt, y_sample)
```
